# Optimizing a Trainium2 kernel written in Bass

```python
import math
import jax, jax.numpy as jnp
from jax import lax
import numpy as np

D_MODEL = 2048
BATCH = 2
SEQ = 8192
DEPTH = 1

N_META = 16
D_MIX = D_MODEL
Q_BLOCK = 128
RMS_EPS = 1e-6
ROPE_THETA = 10000.0

A_HEADS = 8
A_QK_DIM = 64
A_V_DIM = 2 * A_QK_DIM
A_WIDTH = A_HEADS * A_V_DIM
B_HEADS = 8
B_Q_LORA = 512
B_KV_LORA = 256
B_NOPE = 128
B_ROPE = 64
B_V_DIM = 128
B_WIDTH = B_HEADS * B_V_DIM

A_Q_COLS = A_HEADS * 2 * A_QK_DIM
A_K_COLS = A_HEADS * 2 * A_QK_DIM
A_V_COLS = A_WIDTH
A_G_COLS = A_WIDTH
B_CQ_COLS = B_Q_LORA
B_CKV_COLS = B_KV_LORA
B_KR_COLS = B_ROPE
B_G_COLS = B_WIDTH
IN_COLS = A_Q_COLS + A_K_COLS + A_V_COLS + A_G_COLS + B_CQ_COLS + B_CKV_COLS + B_KR_COLS + B_G_COLS
UQ_COLS = B_HEADS * (B_NOPE + B_ROPE)
UKV_COLS = B_HEADS * (B_NOPE + B_V_DIM)

kernel_name = "hymba_diffattn_mla_hybrid"


def _rmsnorm(x, w):
    xf = x.astype(jnp.float32)
    xf = xf * lax.rsqrt(jnp.mean(xf * xf, axis=-1, keepdims=True) + RMS_EPS)
    return xf.astype(x.dtype) * w


def _rope(x, pos):
    half = x.shape[-1] // 2
    inv = ROPE_THETA ** (-jnp.arange(half, dtype=jnp.float32) / half)
    ang = pos.astype(jnp.float32)[:, None] * inv[None, :]
    cos, sin = jnp.cos(ang), jnp.sin(ang)
    xf = x.astype(jnp.float32)
    x1, x2 = xf[..., :half], xf[..., half:]
    return jnp.concatenate([x1 * cos - x2 * sin, x1 * sin + x2 * cos], axis=-1).astype(x.dtype)


def _sweep(block_fn, n_blocks):
    out = lax.map(block_fn, jnp.arange(n_blocks, dtype=jnp.int32) * Q_BLOCK)
    nb, b, h, qb, dv = out.shape
    return jnp.moveaxis(out, 0, 2).reshape(b, h, nb * qb, dv)


def _diff_attention(q, k, v, lam, slopes):
    L = k.shape[3]
    scale = A_QK_DIM ** -0.5
    k_pos = jnp.arange(L, dtype=jnp.int32)

    def block(start):
        qb = lax.dynamic_slice_in_dim(q, start, Q_BLOCK, axis=3)
        rel = (start + jnp.arange(Q_BLOCK, dtype=jnp.int32))[:, None] - k_pos[None, :]
        s = jnp.einsum('bhcqd,bhckd->bhcqk', qb, k).astype(jnp.float32) * scale
        s = s - slopes[:, None, None, None] * rel.astype(jnp.float32)
        s = jnp.where(rel >= 0, s, -jnp.inf)
        p = jax.nn.softmax(s, axis=-1)
        w = p[:, :, 0] - lam * p[:, :, 1]
        return jnp.einsum('bhqk,bhkd->bhqd', w.astype(v.dtype), v)

    return _sweep(block, L // Q_BLOCK)


def _mla_attention(q_nope, q_pe, k_nope, k_pe, v):
    L = k_nope.shape[2]
    scale = (B_NOPE + B_ROPE) ** -0.5
    k_pos = jnp.arange(L, dtype=jnp.int32)

    def block(start):
        qn = lax.dynamic_slice_in_dim(q_nope, start, Q_BLOCK, axis=2)
        qp = lax.dynamic_slice_in_dim(q_pe, start, Q_BLOCK, axis=2)
        rel = (start + jnp.arange(Q_BLOCK, dtype=jnp.int32))[:, None] - k_pos[None, :]
        s = (jnp.einsum('bhqd,bhkd->bhqk', qn, k_nope) + jnp.einsum('bhqd,bkd->bhqk', qp, k_pe)).astype(jnp.float32) * scale
        s = jnp.where(rel >= 0, s, -jnp.inf)
        p = jax.nn.softmax(s, axis=-1)
        return jnp.einsum('bhqk,bhkd->bhqd', p.astype(v.dtype), v)

    return _sweep(block, L // Q_BLOCK)


def setup_inputs(seed: int = 0) -> dict:
    key = jax.random.key(seed)
    ks = jax.random.split(key, 13)
    f32 = jnp.float32
    n = lambda k, shp: jax.random.normal(k, shp, dtype=f32)
    return {
        "x": n(ks[0], (BATCH, SEQ, D_MODEL)),
        "meta_tokens": n(ks[1], (N_META, D_MODEL)),
        "attn_norm_w": 1.0 + 0.02 * n(ks[2], (DEPTH, D_MODEL)),
        "w_in": n(ks[3], (DEPTH, D_MODEL, IN_COLS)) * D_MODEL ** -0.5,
        "diff_lambda": 0.1 * n(ks[4], (DEPTH, 4, A_QK_DIM)),
        "diff_subln_w": 1.0 + 0.02 * n(ks[5], (DEPTH, A_V_DIM)),
        "mla_q_norm_w": 1.0 + 0.02 * n(ks[6], (DEPTH, B_Q_LORA)),
        "w_uq": n(ks[7], (DEPTH, B_Q_LORA, UQ_COLS)) * B_Q_LORA ** -0.5,
        "mla_kv_norm_w": 1.0 + 0.02 * n(ks[8], (DEPTH, B_KV_LORA)),
        "w_ukv": n(ks[9], (DEPTH, B_KV_LORA, UKV_COLS)) * B_KV_LORA ** -0.5,
        "w_out": n(ks[10], (DEPTH, D_MIX, D_MODEL)) * D_MIX ** -0.5,
        "final_norm_w": 1.0 + 0.02 * n(ks[11], (D_MODEL,)),
    }


def reference(x, meta_tokens, attn_norm_w, w_in, diff_lambda, diff_subln_w,
              mla_q_norm_w, w_uq, mla_kv_norm_w, w_ukv, w_out, final_norm_w):
    B, S, D = x.shape
    L = S + N_META
    L_pad = -(-L // Q_BLOCK) * Q_BLOCK
    meta = jnp.broadcast_to(meta_tokens[None].astype(x.dtype), (B, N_META, D))
    h = jnp.concatenate([meta, x], axis=1)
    h = jnp.pad(h, ((0, 0), (0, L_pad - L), (0, 0)))
    pos = jnp.arange(L_pad, dtype=jnp.int32)
    slopes = jnp.exp2(-8.0 * jnp.arange(1, A_HEADS + 1, dtype=jnp.float32) / A_HEADS)
    splits = np.cumsum([A_Q_COLS, A_K_COLS, A_V_COLS, A_G_COLS, B_CQ_COLS, B_CKV_COLS, B_KR_COLS])

    for l in range(DEPTH):
        hn = _rmsnorm(h, attn_norm_w[l])
        proj = hn @ w_in[l]
        a_q, a_k, a_v, a_g, b_cq, b_ckv, b_kr, b_g = jnp.split(proj, list(splits), axis=-1)

        qA = a_q.reshape(B, L_pad, A_HEADS, 2, A_QK_DIM).transpose(0, 2, 3, 1, 4)
        kA = a_k.reshape(B, L_pad, A_HEADS, 2, A_QK_DIM).transpose(0, 2, 3, 1, 4)
        vA = a_v.reshape(B, L_pad, A_HEADS, A_V_DIM).transpose(0, 2, 1, 3)
        lam_init = 0.8 - 0.6 * math.exp(-0.3 * l)
        lp = diff_lambda[l].astype(jnp.float32)
        lam = jnp.exp(jnp.sum(lp[0] * lp[1])) - jnp.exp(jnp.sum(lp[2] * lp[3])) + lam_init
        oA = _diff_attention(qA, kA, vA, lam, slopes)
        oA = _rmsnorm(oA, diff_subln_w[l]) * (1.0 - lam_init)
        oA = oA.transpose(0, 2, 1, 3).reshape(B, L_pad, A_WIDTH) * jax.nn.silu(a_g)

        qB = (_rmsnorm(b_cq, mla_q_norm_w[l]) @ w_uq[l]).reshape(B, L_pad, B_HEADS, B_NOPE + B_ROPE).transpose(0, 2, 1, 3)
        q_nope, q_pe = qB[..., :B_NOPE], _rope(qB[..., B_NOPE:], pos)
        kvB = (_rmsnorm(b_ckv, mla_kv_norm_w[l]) @ w_ukv[l]).reshape(B, L_pad, B_HEADS, B_NOPE + B_V_DIM).transpose(0, 2, 1, 3)
        k_nope, vB = kvB[..., :B_NOPE], kvB[..., B_NOPE:]
        k_pe = _rope(b_kr, pos)
        oB = _mla_attention(q_nope, q_pe, k_nope, k_pe, vB)
        oB = oB.transpose(0, 2, 1, 3).reshape(B, L_pad, B_WIDTH) * jax.nn.silu(b_g)

        h = h + jnp.concatenate([oA, oB], axis=-1) @ w_out[l]

    y = _rmsnorm(h, final_norm_w)
    return y[:, N_META:N_META + S]
```

```python
import contextlib
import numpy as np
import concourse.bass as bass
import concourse.mybir as mybir
from concourse.bass_utils import run_bass_kernel_spmd

F32 = mybir.dt.float32
BF16 = mybir.dt.bfloat16
ALU = mybir.AluOpType
AF = mybir.ActivationFunctionType

NEG = -30000.0
RMS_EPS = 1e-6
ROPE_THETA = 10000.0


class Cfg:
    def __init__(self, D=2048, S=8192, H=8, QL=512, KVL=256, B=2):
        self.D, self.S, self.H, self.QL, self.KVL, self.B = D, S, H, QL, KVL, B
        self.C = D // 128
        self.NX = S // 128
        self.RB = 4
        self.NOWN = self.NX // self.RB
        self.NKB = self.NX + 1
        self.NG = (self.NOWN + 3) // 4
        self.NKSB = (self.NKB + 3) // 4
        self.CQ = QL // 128
        self.CKV = KVL // 128
        self.ncores = self.RB * B


class Tok:
    __slots__ = ("sem", "val", "key", "slot")

    def __init__(self, sem, val, key, slot=None):
        self.sem, self.val, self.key, self.slot = sem, val, key, slot


class Buf:
    __slots__ = ("name", "w", "r")

    def __init__(self, name):
        self.name, self.w, self.r = name, None, {}


class Sched:
    ROT = 30000

    def __init__(self, nc, es):
        self.nc, self.es = nc, es
        self.eng = dict(pe=nc.tensor, act=nc.scalar, dve=nc.vector, pool=nc.gpsimd, sp=nc.sync)
        self.prog = {}
        self.seen = {e: {} for e in self.eng}
        self.nsem = 0
        self.old = []
        for e in self.eng:
            self._new_prog(e)
        self.slots = []
        self.pending = {e: False for e in self.eng}

    def _sem(self, name):
        self.nsem += 1
        return self.es.enter_context(self.nc.semaphore(name))

    def _new_prog(self, e):
        if e in self.prog and self.prog[e][1] > 0:
            self.old.append(self.prog[e])
        k = f"p_{e}_{self.nsem}"
        self.prog[e] = [self._sem(k), 0, k]

    def slot(self, name):
        k = f"d_{name}_{self.nsem}"
        s = [self._sem(k), 0, k]
        self.slots.append(s)
        return s

    def _wait(self, e, tok):
        if tok is None:
            return
        val = tok.slot[1] if tok.slot is not None else tok.val
        if self.seen[e].get(tok.key, 0) >= val:
            return
        self.eng[e].wait_ge(tok.sem, val)
        self.seen[e][tok.key] = val

    def _deps(self, e, reads, writes):
        own = self.prog[e][2] if e == "pe" else None
        for b in reads:
            if b.w is not None and b.w.key != own:
                self._wait(e, b.w)
        for b in writes:
            if b.w is not None and b.w.key != own:
                self._wait(e, b.w)
            for t in b.r.values():
                if t.key != own:
                    self._wait(e, t)

    def _mark(self, tok, reads, writes):
        for b in writes:
            b.w, b.r = tok, {}
        for b in reads:
            b.r[tok.key] = tok

    def op(self, e, fn, reads=(), writes=(), signal=True):
        self._deps(e, reads, writes)
        ins = fn(self.eng[e])
        p = self.prog[e]
        if signal:
            p[1] += 1
            ins.then_inc(p[0], 1)
            tok = Tok(p[0], p[1], p[2])
            self.pending[e] = False
        else:
            tok = Tok(p[0], p[1] + 1, p[2])
            self.pending[e] = True
        self._mark(tok, reads, writes)
        if signal and p[1] >= self.ROT:
            self._new_prog(e)
        return tok

    def dma(self, q, slot, out, in_, reads=(), writes=()):
        self._deps(q, reads, writes)
        ins = self.eng[q].dma_start(out=out, in_=in_)
        slot[1] += 16
        ins.then_inc(slot[0], 16)
        tok = Tok(slot[0], slot[1], slot[2], slot)
        self._mark(tok, reads, writes)
        return tok

    def all_toks(self):
        toks = []
        for e, p in self.prog.items():
            assert not self.pending[e], f"engine {e} has unsignalled ops at barrier"
            if p[1] > 0:
                toks.append(Tok(p[0], p[1], p[2]))
        for s in self.slots:
            if s[1] > 0:
                toks.append(Tok(s[0], s[1], s[2]))
        return toks

    def barrier(self):
        toks = self.all_toks()
        for e in self.eng:
            for t in toks:
                self._wait(e, t)


def build_nc(cfg):
    D, C, H, QL, KVL = cfg.D, cfg.C, cfg.H, cfg.QL, cfg.KVL
    NOWN, NKB, NG, NKSB, CQ, CKV = cfg.NOWN, cfg.NKB, cfg.NG, cfg.NKSB, cfg.CQ, cfg.CKV
    NT = NKB * 128
    NOT = NOWN * 128
    NBX = 4 * NOWN + 4
    nc = bass.Bass("TRN2", target_bir_lowering=False)

    def din(name, shape, dt=F32):
        return nc.dram_tensor(name, list(shape), dt, kind="ExternalInput").ap()

    x_all = din("x_all", [cfg.S, D])
    x_own = din("x_own", [NOT, D])
    meta_pad = din("meta_pad", [128, D])
    wA_d = din("wA", [H, 128, C, 512])
    wckvkr_d = din("wckvkr", [128, C, KVL + 64])
    wcq_d = din("wcq", [128, C, QL])
    wbg_d = din("wbg", [H, 128, C, 128])
    wuq_d = din("wuq", [H, 128, CQ, 256])
    wukv_d = din("wukv", [H, 128, CKV, 256])
    wout_d = din("wout", [128, 2 * H, D])
    cosk_d = din("cosk", [128, NKB, 32])
    sink_d = din("sink", [128, NKB, 32])
    cosq_d = din("cosq", [64, NOT])
    sinq_d = din("sinq", [64, NOT])
    biasx_d = din("biasx", [128, H, NBX])
    biasm_d = din("biasm", [128, H, NOWN])
    biasmb_d = din("biasmb", [128, 1])
    masks_d = din("masks", [128, 4, 128])
    ident_d = din("ident", [128, 128])
    anw_d = din("attn_norm_w", [1, D])
    qnw_d = din("q_norm_w", [1, QL])
    kvnw_d = din("kv_norm_w", [1, KVL])
    fnw_d = din("final_norm_w", [1, D])
    subw_d = din("subln_w", [1, 128])
    lam_d = din("diff_lambda", [1, 256])
    y_d = nc.dram_tensor("y_own", [NOT, D], F32, kind="ExternalOutput").ap()
    hnT_all = nc.dram_tensor("hnT_all", [NKSB, 128, C, 512], BF16).ap()
    hnT_own = nc.dram_tensor("hnT_own", [NG, 128, C, 512], BF16).ap()
    oT_d = nc.dram_tensor("oT_scr", [NG, 128, 2 * H, 512], BF16).ap()

    es = contextlib.ExitStack()
    with es:
        S = Sched(nc, es)
        ctr = {"misc": 0, "s": 0, "pt": 0}

        def sb(stack, name, shape, dt):
            ctr["sb"] = ctr.get("sb", 0) + 1
            return stack.enter_context(nc.sbuf_tensor(f"s{ctr['sb']}_{name}", list(shape), dt))

        def bc(ap_1xn):
            return ap_1xn.partition_broadcast(128)

        ps = [es.enter_context(nc.psum_tensor(f"ps{i}", [128, 512], F32)) for i in range(8)]
        psb = [Buf(f"ps{i}") for i in range(8)]
        PS_S = [0, 1]
        PS_O = [2, 3, 4, 5]
        PS_M = [6, 7]

        def misc():
            i = PS_M[ctr["misc"] % 2]
            ctr["misc"] += 1
            return i

        ident = sb(es, "ident", [128, 128], BF16)
        masks = sb(es, "masks", [128, 4, 128], BF16)
        cst32 = sb(es, "cst32", [128, 640], F32)
        lamt = sb(es, "lamt", [128, 256], F32)
        lamj = sb(es, "lamj", [128, 64], F32)
        lams = sb(es, "lams", [128, 8], F32)
        subw = sb(es, "subw", [128, 128], F32)
        biasx = sb(es, "biasx", [128, H, NBX], F32)
        biasm = sb(es, "biasm", [128, H, NOWN], F32)
        biasmb = sb(es, "biasmb", [128, 1], F32)
        epsb = sb(es, "epsb", [128, 1], F32)
        b_const = Buf("const")
        sl_c = S.slot("const")
        S.dma("sp", sl_c, cst32[:, 0:512], masks_d.rearrange("p a b -> p (a b)"), writes=[b_const])
        S.dma("sp", sl_c, cst32[:, 512:640], ident_d[:, :], writes=[b_const])
        S.dma("sp", sl_c, lamt[:, :], bc(lam_d), writes=[b_const])
        S.dma("sp", sl_c, subw[:, :], bc(subw_d), writes=[b_const])
        S.dma("sp", sl_c, biasx[:, :, :], biasx_d[:, :, :], writes=[b_const])
        S.dma("sp", sl_c, biasm[:, :, :], biasm_d[:, :, :], writes=[b_const])
        S.dma("sp", sl_c, biasmb[:, :], biasmb_d[:, :], writes=[b_const])
        b_c2 = Buf("const2")
        S.op("pool", lambda e: e.memset(epsb[:, :], RMS_EPS), writes=[b_c2])
        S.op("dve", lambda e: e.tensor_copy(out=masks[:, :, :].rearrange("p a b -> p (a b)"), in_=cst32[:, 0:512]),
             reads=[b_const], writes=[b_c2])
        S.op("dve", lambda e: e.tensor_copy(out=ident[:, :], in_=cst32[:, 512:640]), reads=[b_const, b_c2], writes=[b_c2])
        b_lam = Buf("lam")
        S.op("dve", lambda e: e.scalar_tensor_tensor(out=lamj[:, :], in0=lamt[:, 0:64], scalar=1.0, in1=lamt[:, 64:128],
                                                     op0=ALU.mult, op1=ALU.mult, accum_out=lams[:, 0:1]),
             reads=[b_const], writes=[b_lam])
        S.op("dve", lambda e: e.scalar_tensor_tensor(out=lamj[:, :], in0=lamt[:, 128:192], scalar=1.0, in1=lamt[:, 192:256],
                                                     op0=ALU.mult, op1=ALU.mult, accum_out=lams[:, 1:2]),
             reads=[b_const, b_lam], writes=[b_lam])
        S.op("act", lambda e: e.activation(out=lams[:, 2:4], in_=lams[:, 0:2], func=AF.Exp), reads=[b_lam], writes=[b_lam])
        S.op("dve", lambda e: e.tensor_tensor(out=lams[:, 4:5], in0=lams[:, 2:3], in1=lams[:, 3:4], op=ALU.subtract),
             reads=[b_lam], writes=[b_lam])
        S.op("dve", lambda e: e.tensor_scalar(out=lams[:, 5:6], in0=lams[:, 4:5], scalar1=0.2, scalar2=-1.0,
                                              op0=ALU.add, op1=ALU.mult), reads=[b_lam], writes=[b_lam])
        S.op("dve", lambda e: e.tensor_scalar(out=subw[:, :], in0=subw[:, :], scalar1=0.8, scalar2=None, op0=ALU.mult),
             reads=[b_const, b_c2], writes=[b_c2])

        def rstd_from_ss(ss_ap, out_ap, n, buf):
            S.op("act", lambda e: e.activation(out=out_ap, in_=ss_ap, func=AF.Ln, scale=1.0 / n, bias=epsb[:, 0:1]),
                 reads=[buf, b_c2], writes=[buf])
            S.op("act", lambda e: e.activation(out=out_ap, in_=out_ap, func=AF.Exp, scale=-0.5),
                 reads=[buf], writes=[buf])

        def sumsq(src_ap, junk_ap, acc_ap, rd, wr):
            S.op("dve", lambda e: e.scalar_tensor_tensor(out=junk_ap, in0=src_ap, scalar=1.0, in1=src_ap,
                                                         op0=ALU.mult, op1=ALU.mult, accum_out=acc_ap),
                 reads=rd, writes=wr)

        def transpose_to(src_ap, K, M, psi, col0, rd, signal):
            pv = ps[psi][:, :].bitcast(BF16)
            S.op("pe", lambda e: e.transpose(out=pv[0:M, col0:col0 + K], in_=src_ap, identity=ident[0:K, 0:K]),
                 reads=rd + [b_c2], writes=[psb[psi]], signal=signal)

        def mm(out, lhsT, rhs, start, stop, rd, wr, signal):
            S.op("pe", lambda e: e.matmul(out, lhsT=lhsT, rhs=rhs, start=start, stop=stop),
                 reads=rd, writes=wr, signal=signal)

        def load_w(stack, name, shape, src_ap, slot, buf):
            t = sb(stack, name, shape, BF16)
            S.dma("pool", slot, t[:], src_ap, writes=[buf])
            return t

        with contextlib.ExitStack() as p0:
            anw = sb(p0, "anw", [128, D], F32)
            b_anw = Buf("anw")
            sl_anw = S.slot("anw")
            S.dma("sp", sl_anw, anw[:, :], bc(anw_d), writes=[b_anw])
            xt = [sb(p0, f"xt{i}", [128, D], F32) for i in range(2)]
            xb = [Buf(f"xt{i}") for i in range(2)]
            junk = sb(p0, "junk0", [128, D], BF16)
            jb = Buf("junk0")
            hn = [sb(p0, f"hn{i}", [128, D], BF16) for i in range(2)]
            hb = [Buf(f"hn{i}") for i in range(2)]
            st = sb(p0, "st0", [128, 8], F32)
            stb = [Buf("st0a"), Buf("st0b")]
            hT = [sb(p0, f"hT{i}", [128, C, 512], BF16) for i in range(2)]
            hTb = [Buf(f"hT{i}") for i in range(2)]
            sl_x = [S.slot("x0"), S.slot("x1")]
            sl_h = [S.slot("h0"), S.slot("h1")]
            blocks = [("k", kb) for kb in range(NKB)] + [("o", ob) for ob in range(NOWN)]
            nsb, cnt, cur = 0, 0, 0
            for kind, bi in blocks:
                i2 = cnt % 2
                sbi, bl = bi // 4, bi % 4
                if kind == "k":
                    src = meta_pad[:, :] if bi == 0 else x_all[(bi - 1) * 128: bi * 128, :]
                    nblk = min(4, NKB - sbi * 4)
                    dst = hnT_all
                else:
                    src = x_own[bi * 128:(bi + 1) * 128, :]
                    nblk = min(4, NOWN - sbi * 4)
                    dst = hnT_own
                if bl == 0:
                    cur = nsb % 2
                    nsb += 1
                S.dma("sp", sl_x[i2], xt[i2][:, :], src, writes=[xb[i2]])
                sumsq(xt[i2][:, :], junk[:, :], st[:, i2:i2 + 1], [xb[i2]], [jb, stb[i2]])
                rstd_from_ss(st[:, i2:i2 + 1], st[:, 2 + i2:3 + i2], D, stb[i2])
                S.op("dve", lambda e: e.scalar_tensor_tensor(out=hn[i2][:, :], in0=xt[i2][:, :], scalar=st[:, 2 + i2:3 + i2],
                                                             in1=anw[:, :], op0=ALU.mult, op1=ALU.mult),
                     reads=[xb[i2], stb[i2], b_anw], writes=[hb[i2]])
                for c0 in range(0, C, 8):
                    ncz = min(8, C - c0)
                    psi = misc()
                    for c in range(c0, c0 + ncz):
                        transpose_to(hn[i2][:, c * 128:(c + 1) * 128], 128, 128, psi, (c - c0) * 128, [hb[i2]],
                                     signal=(c == c0 + ncz - 1))
                    pv = ps[psi][:, :].bitcast(BF16)
                    src_v = pv[:, 0:ncz * 128].rearrange("p (c t) -> p c t", c=ncz)
                    dst_v = hT[cur][:, c0:c0 + ncz, bl * 128:(bl + 1) * 128]
                    S.op("dve", lambda e: e.tensor_copy(out=dst_v, in_=src_v), reads=[psb[psi]], writes=[hTb[cur]])
                if bl == nblk - 1:
                    S.dma("sp", sl_h[cur], dst[sbi, :, :, 0:nblk * 128], hT[cur][:, :, 0:nblk * 128], reads=[hTb[cur]])
                cnt += 1
        S.barrier()

        with contextlib.ExitStack() as pa:
            vaug = sb(pa, "vaug", [128, NKB, 130], BF16)
            vb = [Buf(f"v{i}") for i in range(NKSB)]
            b_ones = Buf("ones")
            S.op("pool", lambda e: e.memset(vaug[:, :, 128:130], 1.0), writes=[b_ones])
            kT = sb(pa, "kT", [128, NT], BF16)
            kb_ = [Buf(f"k{i}") for i in range(NKSB)]
            hs = [sb(pa, f"hs{i}", [128, C, 512], BF16) for i in range(2)]
            hsb = [Buf(f"hs{i}") for i in range(2)]
            sl_hs = [S.slot("hs0"), S.slot("hs1")]
            ctr["hs"] = 0
            pT = [sb(pa, f"pT{i}", [128, 512], BF16) for i in range(3)]
            pTb = [Buf(f"pT{i}") for i in range(3)]
            qT = sb(pa, "qT", [128, 512], BF16)
            qTb = Buf("qT")
            gsil = sb(pa, "gsil", [128, 4, 128], F32)
            gsb = Buf("gsil")
            gtmp = sb(pa, "gtmp", [128, 512], F32)
            gtb = Buf("gtmp")
            t0 = sb(pa, "t0", [128, 4, 128], F32)
            t0b = [Buf(f"t0_{i}") for i in range(4)]
            dd_ = sb(pa, "dd", [128, 2, 128], F32)
            ddb = [Buf("dd0"), Buf("dd1")]
            ejunk = sb(pa, "ejunk", [128, 128], F32)
            ejb = Buf("ejunk")
            est = sb(pa, "est", [128, 2, 8], F32)
            estb = [Buf("est0"), Buf("est1")]
            obf = sb(pa, "obf", [128, 2, 128], BF16)
            obb = [Buf("obf0"), Buf("obf1")]
            oTs = [sb(pa, f"oTs{i}", [128, 512], BF16) for i in range(2)]
            oTb = [Buf(f"oTs{i}") for i in range(2)]
            sl_o = [S.slot("o0"), S.slot("o1")]
            ctr["ep"] = 0
            ctr["ot"] = 0
            sl_w = S.slot("w")

            def stream_hs(src_ap, ncols):
                i = ctr["hs"] % 2
                ctr["hs"] += 1
                S.dma("sp", sl_hs[i], hs[i][:, :, 0:ncols], src_ap, writes=[hsb[i]])
                return i

            def attention_pass(g, parts, scale, bias_fn, per_block_act, epilogue):
                nbq = min(4, NOWN - 4 * g)
                jmax = 4 * (4 * g + nbq - 1) + 3
                for kb in range(0, jmax + 2):
                    j = kb - 1
                    qs = 0 if kb == 0 else max(0, j // 4 - 4 * g)
                    has_mask = kb > 0 and (j // 4 - 4 * g) >= 0
                    si = PS_S[ctr["s"] % 2]
                    ctr["s"] += 1
                    pi = ctr["pt"] % 3
                    ctr["pt"] += 1
                    c0, c1 = qs * 128, nbq * 128
                    for ip, (kfn, qfn, rdb) in enumerate(parts):
                        last = (ip == len(parts) - 1) and not has_mask
                        mm(ps[si][:, c0:c1], kfn(kb), qfn(c0, c1), ip == 0, last,
                           rdb + [kb_[kb // 4], qTb], [psb[si]], signal=last)
                    if has_mask:
                        mm(ps[si][:, c0:c0 + 128], ident[:, :], masks[:, j % 4, :], False, True,
                           [b_c2], [psb[si]], signal=True)
                    if per_block_act:
                        for qb in range(qs, nbq):
                            S.op("act", lambda e: e.activation(out=pT[pi][:, qb * 128:(qb + 1) * 128],
                                                               in_=ps[si][:, qb * 128:(qb + 1) * 128], func=AF.Exp,
                                                               bias=bias_fn(kb, qb), scale=scale),
                                 reads=[psb[si], b_const], writes=[pTb[pi]])
                    else:
                        S.op("act", lambda e: e.activation(out=pT[pi][:, c0:c1], in_=ps[si][:, c0:c1], func=AF.Exp,
                                                           bias=bias_fn(kb, qs), scale=scale),
                             reads=[psb[si], b_const], writes=[pTb[pi]])
                    for qb in range(qs, nbq):
                        lastkb = (kb == 4 * (4 * g + qb) + 3 + 1)
                        mm(ps[PS_O[qb]][:, 0:129], pT[pi][:, qb * 128:(qb + 1) * 128], vaug[:, kb, 0:129],
                           kb == 0, lastkb, [pTb[pi], vb[kb // 4], b_ones], [psb[PS_O[qb]]], signal=(qb == nbq - 1))
                    for qb in range(qs, nbq):
                        if kb == 4 * (4 * g + qb) + 3 + 1:
                            epilogue(qb)

            def finish_o(g, qb, head_idx, o_src_fn):
                nbq = min(4, NOWN - 4 * g)
                k = ctr["ep"] % 2
                ctr["ep"] += 1
                o_src_fn(obf[:, k, :], obb[k])
                psi = misc()
                transpose_to(obf[:, k, :], 128, 128, psi, 0, [obb[k]], signal=True)
                oi = ctr["ot"] % 2
                pv = ps[psi][:, :].bitcast(BF16)
                S.op("dve", lambda e: e.tensor_copy(out=oTs[oi][:, qb * 128:(qb + 1) * 128], in_=pv[:, 0:128]),
                     reads=[psb[psi]], writes=[oTb[oi]])
                if qb == nbq - 1:
                    S.dma("sp", sl_o[oi], oT_d[g, :, head_idx, 0:nbq * 128], oTs[oi][:, 0:nbq * 128], reads=[oTb[oi]])
                    ctr["ot"] += 1

            def gate_proj(hi_buf, nbq, w_ap_fn):
                psi = misc()
                for blk in range(nbq):
                    for c in range(C):
                        mm(ps[psi][:, blk * 128:(blk + 1) * 128], hs[hi_buf][:, c, blk * 128:(blk + 1) * 128], w_ap_fn(c),
                           c == 0, c == C - 1, [hsb[hi_buf], b_w], [psb[psi]], signal=(c == C - 1 and blk == nbq - 1))
                n = nbq * 128
                S.op("act", lambda e: e.activation(out=gtmp[:, 0:n], in_=ps[psi][:, 0:n], func=AF.Exp, scale=-1.0),
                     reads=[psb[psi]], writes=[gtb])
                S.op("dve", lambda e: e.tensor_scalar(out=gtmp[:, 0:n], in0=gtmp[:, 0:n], scalar1=1.0, scalar2=None, op0=ALU.add),
                     reads=[gtb], writes=[gtb])
                S.op("dve", lambda e: e.reciprocal(out=gtmp[:, 0:n], in_=gtmp[:, 0:n]), reads=[gtb], writes=[gtb])
                return psi

            b_w = Buf("w")
            for h in range(H):
                with contextlib.ExitStack() as ph:
                    wA = load_w(ph, "wA_sb", [128, C, 512], wA_d[h], sl_w, b_w)
                    for ksb in range(NKSB):
                        nblk = min(4, NKB - 4 * ksb)
                        n = nblk * 128
                        hi_ = stream_hs(hnT_all[ksb, :, :, 0:n], n)
                        psi = misc()
                        for c in range(C):
                            mm(ps[psi][:, 0:n], wA[:, c, 128:256], hs[hi_][:, c, 0:n], c == 0, c == C - 1,
                               [hsb[hi_], b_w], [psb[psi]], signal=(c == C - 1))
                        S.op("dve", lambda e: e.tensor_copy(out=kT[:, ksb * 512:ksb * 512 + n], in_=ps[psi][:, 0:n]),
                             reads=[psb[psi]], writes=[kb_[ksb]])
                        psi = misc()
                        for blk in range(nblk):
                            for c in range(C):
                                mm(ps[psi][:, blk * 128:(blk + 1) * 128], hs[hi_][:, c, blk * 128:(blk + 1) * 128],
                                   wA[:, c, 256:384], c == 0, c == C - 1, [hsb[hi_], b_w], [psb[psi]],
                                   signal=(c == C - 1 and blk == nblk - 1))
                        S.op("dve", lambda e: e.tensor_copy(out=vaug[:, 4 * ksb:4 * ksb + nblk, 0:128],
                                                            in_=ps[psi][:, 0:n].rearrange("p (b d) -> p b d", b=nblk)),
                             reads=[psb[psi]], writes=[vb[ksb]])
                    for g in range(NG):
                        nbq = min(4, NOWN - 4 * g)
                        n = nbq * 128
                        hi_ = stream_hs(hnT_own[g, :, :, 0:n], n)
                        psi = misc()
                        for c in range(C):
                            mm(ps[psi][:, 0:n], wA[:, c, 0:128], hs[hi_][:, c, 0:n], c == 0, c == C - 1,
                               [hsb[hi_], b_w], [psb[psi]], signal=(c == C - 1))
                        S.op("dve", lambda e: e.tensor_copy(out=qT[:, 0:n], in_=ps[psi][:, 0:n]), reads=[psb[psi]], writes=[qTb])
                        psg = gate_proj(hi_, nbq, lambda c: wA[:, c, 384:512])
                        for blk in range(nbq):
                            S.op("dve", lambda e: e.tensor_tensor(out=gsil[:, blk, :], in0=ps[psg][:, blk * 128:(blk + 1) * 128],
                                                                  in1=gtmp[:, blk * 128:(blk + 1) * 128], op=ALU.mult),
                                 reads=[psb[psg], gtb], writes=[gsb])
                            S.op("dve", lambda e: e.tensor_tensor(out=gsil[:, blk, :], in0=gsil[:, blk, :], in1=subw[:, :], op=ALU.mult),
                                 reads=[gsb, b_c2], writes=[gsb])

                        def bias_fn(kb, qb, h=h, g=g):
                            m = 4 * g + qb
                            if kb == 0:
                                return biasm[:, h, m:m + 1]
                            ddx = 4 * m - (kb - 1) + 3
                            return biasx[:, h, ddx:ddx + 1]

                        def epi0(qb):
                            k = ctr["ep"] % 2
                            acc = ps[PS_O[qb]]
                            S.op("dve", lambda e: e.reciprocal(out=est[:, k, 0:1], in_=acc[:, 128:129]),
                                 reads=[psb[PS_O[qb]]], writes=[estb[k]])
                            S.op("dve", lambda e: e.tensor_scalar(out=t0[:, qb, :], in0=acc[:, 0:128], scalar1=est[:, k, 0:1],
                                                                  scalar2=None, op0=ALU.mult),
                                 reads=[psb[PS_O[qb]], estb[k]], writes=[t0b[qb]])
                            ctr["ep"] += 1

                        def epi1(qb, h=h, g=g):
                            k = ctr["ep"] % 2
                            acc = ps[PS_O[qb]]
                            S.op("dve", lambda e: e.reciprocal(out=est[:, k, 0:1], in_=acc[:, 128:129]),
                                 reads=[psb[PS_O[qb]]], writes=[estb[k]])
                            S.op("dve", lambda e: e.tensor_tensor(out=est[:, k, 1:2], in0=est[:, k, 0:1], in1=lams[:, 5:6], op=ALU.mult),
                                 reads=[estb[k], b_lam], writes=[estb[k]])
                            S.op("dve", lambda e: e.scalar_tensor_tensor(out=dd_[:, k, :], in0=acc[:, 0:128], scalar=est[:, k, 1:2],
                                                                         in1=t0[:, qb, :], op0=ALU.mult, op1=ALU.add),
                                 reads=[psb[PS_O[qb]], estb[k], t0b[qb]], writes=[ddb[k]])
                            S.op("dve", lambda e: e.scalar_tensor_tensor(out=ejunk[:, :], in0=dd_[:, k, :], scalar=1.0, in1=dd_[:, k, :],
                                                                         op0=ALU.mult, op1=ALU.mult, accum_out=est[:, k, 2:3]),
                                 reads=[ddb[k], estb[k]], writes=[ejb, estb[k]])
                            rstd_from_ss(est[:, k, 2:3], est[:, k, 3:4], 128, estb[k])

                            def mk(out_ap, out_buf):
                                S.op("dve", lambda e: e.scalar_tensor_tensor(out=out_ap, in0=dd_[:, k, :], scalar=est[:, k, 3:4],
                                                                             in1=gsil[:, qb, :], op0=ALU.mult, op1=ALU.mult),
                                     reads=[ddb[k], estb[k], gsb], writes=[out_buf])
                            finish_o(g, qb, h, mk)

                        for comp in range(2):
                            r0 = 64 * comp
                            attention_pass(
                                g,
                                [(lambda kb, r0=r0: kT[r0:r0 + 64, kb * 128:(kb + 1) * 128],
                                  lambda c0, c1, r0=r0: qT[r0:r0 + 64, c0:c1], [])],
                                0.125, bias_fn, True, epi0 if comp == 0 else epi1)
            S.barrier()

            with contextlib.ExitStack() as pb:
                ckvT = sb(pb, "ckvT", [128, CKV, NT], BF16)
                kpeT = sb(pb, "kpeT", [64, NT], BF16)
                cqT = sb(pb, "cqT", [128, CQ, NOT], BF16)
                b_ckv = [Buf(f"ckv{i}") for i in range(NKSB)]
                b_cq = [Buf(f"cq{i}") for i in range(NG)]
                b_tab = Buf("tabs")
                sl_t = S.slot("tabs")
                with contextlib.ExitStack() as pp:
                    wck = load_w(pp, "wck", [128, C, KVL + 64], wckvkr_d[:, :, :], sl_w, b_w)
                    wcq = load_w(pp, "wcq", [128, C, QL], wcq_d[:, :, :], sl_w, b_w)
                    cosk = sb(pp, "cosk", [128, NKB, 32], F32)
                    sink = sb(pp, "sink", [128, NKB, 32], F32)
                    kvnw = sb(pp, "kvnw", [128, KVL], F32)
                    qnw = sb(pp, "qnw", [128, QL], F32)
                    S.dma("sp", sl_t, cosk[:, :, :], cosk_d[:, :, :], writes=[b_tab])
                    S.dma("sp", sl_t, sink[:, :, :], sink_d[:, :, :], writes=[b_tab])
                    S.dma("sp", sl_t, kvnw[:, :], bc(kvnw_d), writes=[b_tab])
                    S.dma("sp", sl_t, qnw[:, :], bc(qnw_d), writes=[b_tab])
                    pj = sb(pp, "pj", [128, 512], BF16)
                    pjb = Buf("pj")
                    pf = sb(pp, "pf", [128, 512], F32)
                    pfb = Buf("pf")
                    pst = sb(pp, "pst", [128, 2, 4], F32)
                    pstb = [Buf("pst0"), Buf("pst1")]
                    cn = sb(pp, "cn", [128, 2, 512], BF16)
                    cnb = [Buf("cn0"), Buf("cn1")]
                    rt = sb(pp, "rt", [128, 4, 32], F32)
                    rtb = Buf("rt")
                    k2 = 0
                    for ksb in range(NKSB):
                        nblk = min(4, NKB - 4 * ksb)
                        hi_ = stream_hs(hnT_all[ksb, :, :, 0:nblk * 128], nblk * 128)
                        for blk in range(nblk):
                            kb = 4 * ksb + blk
                            k = k2 % 2
                            k2 += 1
                            psi = misc()
                            for c in range(C):
                                mm(ps[psi][:, 0:KVL + 64], hs[hi_][:, c, blk * 128:(blk + 1) * 128], wck[:, c, :],
                                   c == 0, c == C - 1, [hsb[hi_], b_w], [psb[psi]], signal=(c == C - 1))
                            S.op("dve", lambda e: e.tensor_copy(out=pf[:, 0:KVL], in_=ps[psi][:, 0:KVL]), reads=[psb[psi]], writes=[pfb])
                            sumsq(pf[:, 0:KVL], pj[:, 0:KVL], pst[:, k, 0:1], [pfb], [pjb, pstb[k]])
                            rstd_from_ss(pst[:, k, 0:1], pst[:, k, 1:2], KVL, pstb[k])
                            S.op("dve", lambda e: e.scalar_tensor_tensor(out=cn[:, k, 0:KVL], in0=ps[psi][:, 0:KVL], scalar=pst[:, k, 1:2],
                                                                         in1=kvnw[:, :], op0=ALU.mult, op1=ALU.mult),
                                 reads=[psb[psi], pstb[k], b_tab], writes=[cnb[k]])
                            x1 = ps[psi][:, KVL:KVL + 32]
                            x2 = ps[psi][:, KVL + 32:KVL + 64]
                            S.op("dve", lambda e: e.tensor_tensor(out=rt[:, 0, :], in0=x1, in1=cosk[:, kb, :], op=ALU.mult),
                                 reads=[psb[psi], b_tab], writes=[rtb])
                            S.op("dve", lambda e: e.tensor_tensor(out=rt[:, 1, :], in0=x2, in1=sink[:, kb, :], op=ALU.mult),
                                 reads=[psb[psi], b_tab, rtb], writes=[rtb])
                            S.op("dve", lambda e: e.tensor_tensor(out=rt[:, 2, :], in0=x1, in1=sink[:, kb, :], op=ALU.mult),
                                 reads=[psb[psi], b_tab, rtb], writes=[rtb])
                            S.op("dve", lambda e: e.tensor_tensor(out=rt[:, 3, :], in0=x2, in1=cosk[:, kb, :], op=ALU.mult),
                                 reads=[psb[psi], b_tab, rtb], writes=[rtb])
                            S.op("dve", lambda e: e.tensor_tensor(out=cn[:, k, KVL:KVL + 32], in0=rt[:, 0, :], in1=rt[:, 1, :], op=ALU.subtract),
                                 reads=[rtb, cnb[k]], writes=[cnb[k]])
                            S.op("dve", lambda e: e.tensor_tensor(out=cn[:, k, KVL + 32:KVL + 64], in0=rt[:, 2, :], in1=rt[:, 3, :], op=ALU.add),
                                 reads=[rtb, cnb[k]], writes=[cnb[k]])
                            pst_ = misc()
                            for c in range(CKV):
                                transpose_to(cn[:, k, c * 128:(c + 1) * 128], 128, 128, pst_, c * 128, [cnb[k]], signal=False)
                            transpose_to(cn[:, k, KVL:KVL + 64], 128, 64, pst_, CKV * 128, [cnb[k]], signal=True)
                            pv = ps[pst_][:, :].bitcast(BF16)
                            S.op("dve", lambda e: e.tensor_copy(out=ckvT[:, :, kb * 128:(kb + 1) * 128],
                                                                in_=pv[:, 0:CKV * 128].rearrange("p (c t) -> p c t", c=CKV)),
                                 reads=[psb[pst_]], writes=[b_ckv[ksb]])
                            S.op("dve", lambda e: e.tensor_copy(out=kpeT[0:64, kb * 128:(kb + 1) * 128],
                                                                in_=pv[0:64, CKV * 128:CKV * 128 + 128]),
                                 reads=[psb[pst_]], writes=[b_ckv[ksb]])
                    for g in range(NG):
                        nbq = min(4, NOWN - 4 * g)
                        hi_ = stream_hs(hnT_own[g, :, :, 0:nbq * 128], nbq * 128)
                        for blk in range(nbq):
                            ob = 4 * g + blk
                            k = k2 % 2
                            k2 += 1
                            psi = misc()
                            for c in range(C):
                                mm(ps[psi][:, 0:QL], hs[hi_][:, c, blk * 128:(blk + 1) * 128], wcq[:, c, :],
                                   c == 0, c == C - 1, [hsb[hi_], b_w], [psb[psi]], signal=(c == C - 1))
                            S.op("dve", lambda e: e.tensor_copy(out=pf[:, 0:QL], in_=ps[psi][:, 0:QL]), reads=[psb[psi]], writes=[pfb])
                            sumsq(pf[:, 0:QL], pj[:, 0:QL], pst[:, k, 0:1], [pfb], [pjb, pstb[k]])
                            rstd_from_ss(pst[:, k, 0:1], pst[:, k, 1:2], QL, pstb[k])
                            S.op("dve", lambda e: e.scalar_tensor_tensor(out=cn[:, k, 0:QL], in0=ps[psi][:, 0:QL], scalar=pst[:, k, 1:2],
                                                                         in1=qnw[:, :], op0=ALU.mult, op1=ALU.mult),
                                 reads=[psb[psi], pstb[k], b_tab], writes=[cnb[k]])
                            pst_ = misc()
                            for c in range(CQ):
                                transpose_to(cn[:, k, c * 128:(c + 1) * 128], 128, 128, pst_, c * 128, [cnb[k]], signal=(c == CQ - 1))
                            pv = ps[pst_][:, :].bitcast(BF16)
                            S.op("dve", lambda e: e.tensor_copy(out=cqT[:, :, ob * 128:(ob + 1) * 128],
                                                                in_=pv[:, 0:CQ * 128].rearrange("p (c t) -> p c t", c=CQ)),
                                 reads=[psb[pst_]], writes=[b_cq[g]])
                S.barrier()

                cosq = sb(pb, "cosq", [64, NOT], F32)
                sinq = sb(pb, "sinq", [64, NOT], F32)
                S.dma("sp", sl_t, cosq[:, :], cosq_d[:, :], writes=[b_tab])
                S.dma("sp", sl_t, sinq[:, :], sinq_d[:, :], writes=[b_tab])
                qpe = sb(pb, "qpe", [64, 512], BF16)
                qpeb = Buf("qpe")
                qr = sb(pb, "qr", [64, 2, 512], F32)
                qrb = Buf("qr")
                for h in range(H):
                    with contextlib.ExitStack() as ph:
                        wbg = load_w(ph, "wbg_sb", [128, C, 128], wbg_d[h], sl_w, b_w)
                        wuq = load_w(ph, "wuq_sb", [128, CQ, 256], wuq_d[h], sl_w, b_w)
                        wukv = load_w(ph, "wukv_sb", [128, CKV, 256], wukv_d[h], sl_w, b_w)
                        for ksb in range(NKSB):
                            nblk = min(4, NKB - 4 * ksb)
                            n = nblk * 128
                            t0_ = ksb * 512
                            psi = misc()
                            for c in range(CKV):
                                mm(ps[psi][:, 0:n], wukv[:, c, 0:128], ckvT[:, c, t0_:t0_ + n], c == 0, c == CKV - 1,
                                   [b_ckv[ksb], b_w], [psb[psi]], signal=(c == CKV - 1))
                            S.op("dve", lambda e: e.tensor_copy(out=kT[:, t0_:t0_ + n], in_=ps[psi][:, 0:n]),
                                 reads=[psb[psi]], writes=[kb_[ksb]])
                            psi = misc()
                            for blk in range(nblk):
                                for c in range(CKV):
                                    mm(ps[psi][:, blk * 128:(blk + 1) * 128], ckvT[:, c, t0_ + blk * 128:t0_ + (blk + 1) * 128],
                                       wukv[:, c, 128:256], c == 0, c == CKV - 1, [b_ckv[ksb], b_w], [psb[psi]],
                                       signal=(c == CKV - 1 and blk == nblk - 1))
                            S.op("dve", lambda e: e.tensor_copy(out=vaug[:, 4 * ksb:4 * ksb + nblk, 0:128],
                                                                in_=ps[psi][:, 0:n].rearrange("p (b d) -> p b d", b=nblk)),
                                 reads=[psb[psi]], writes=[vb[ksb]])
                        for g in range(NG):
                            nbq = min(4, NOWN - 4 * g)
                            n = nbq * 128
                            o0 = g * 512
                            hi_ = stream_hs(hnT_own[g, :, :, 0:n], n)
                            psg = gate_proj(hi_, nbq, lambda c: wbg[:, c, :])
                            S.op("dve", lambda e: e.tensor_tensor(out=gsil[:, 0:nbq, :].rearrange("p b d -> p (b d)"),
                                                                  in0=ps[psg][:, 0:n], in1=gtmp[:, 0:n], op=ALU.mult),
                                 reads=[psb[psg], gtb], writes=[gsb])
                            psi = misc()
                            for c in range(CQ):
                                mm(ps[psi][:, 0:n], wuq[:, c, 0:128], cqT[:, c, o0:o0 + n], c == 0, c == CQ - 1,
                                   [b_cq[g], b_w], [psb[psi]], signal=(c == CQ - 1))
                            S.op("dve", lambda e: e.tensor_copy(out=qT[:, 0:n], in_=ps[psi][:, 0:n]), reads=[psb[psi]], writes=[qTb])
                            psi = misc()
                            for c in range(CQ):
                                mm(ps[psi][0:64, 0:n], wuq[:, c, 128:192], cqT[:, c, o0:o0 + n], c == 0, c == CQ - 1,
                                   [b_cq[g], b_w], [psb[psi]], signal=(c == CQ - 1))
                            S.op("dve", lambda e: e.tensor_tensor(out=qr[:, 0, 0:n], in0=ps[psi][0:64, 0:n], in1=cosq[:, o0:o0 + n], op=ALU.mult),
                                 reads=[psb[psi], b_tab], writes=[qrb])
                            psi = misc()
                            for c in range(CQ):
                                mm(ps[psi][0:64, 0:n], wuq[:, c, 192:256], cqT[:, c, o0:o0 + n], c == 0, c == CQ - 1,
                                   [b_cq[g], b_w], [psb[psi]], signal=(c == CQ - 1))
                            S.op("dve", lambda e: e.tensor_tensor(out=qr[:, 1, 0:n], in0=ps[psi][0:64, 0:n], in1=sinq[:, o0:o0 + n], op=ALU.mult),
                                 reads=[psb[psi], b_tab, qrb], writes=[qrb])
                            S.op("dve", lambda e: e.tensor_tensor(out=qpe[:, 0:n], in0=qr[:, 0, 0:n], in1=qr[:, 1, 0:n], op=ALU.add),
                                 reads=[qrb], writes=[qpeb])

                            def bias_b(kb, qb):
                                return biasmb[:, 0:1] if kb == 0 else 0.0

                            def epib(qb, h=h, g=g):
                                k = ctr["ep"] % 2
                                acc = ps[PS_O[qb]]
                                S.op("dve", lambda e: e.reciprocal(out=est[:, k, 0:1], in_=acc[:, 128:129]),
                                     reads=[psb[PS_O[qb]]], writes=[estb[k]])

                                def mk(out_ap, out_buf):
                                    S.op("dve", lambda e: e.scalar_tensor_tensor(out=out_ap, in0=acc[:, 0:128], scalar=est[:, k, 0:1],
                                                                                 in1=gsil[:, qb, :], op0=ALU.mult, op1=ALU.mult),
                                         reads=[psb[PS_O[qb]], estb[k], gsb], writes=[out_buf])
                                finish_o(g, qb, H + h, mk)

                            attention_pass(
                                g,
                                [(lambda kb: kT[:, kb * 128:(kb + 1) * 128], lambda c0, c1: qT[:, c0:c1], []),
                                 (lambda kb: kpeT[0:64, kb * 128:(kb + 1) * 128], lambda c0, c1: qpe[0:64, c0:c1],
                                  [qpeb] + b_ckv)],
                                192.0 ** -0.5, bias_b, False, epib)
                S.barrier()
        S.barrier()

        with contextlib.ExitStack() as p2:
            b_w2 = Buf("w2")
            wout = load_w(p2, "wout_sb", [128, 2 * H, D], wout_d[:, :, :], sl_w, b_w2)
            fnw = sb(p2, "fnw", [128, D], F32)
            sl_f = S.slot("fnw")
            S.dma("sp", sl_f, fnw[:, :], bc(fnw_d), writes=[b_w2])
            oin = [sb(p2, f"oin{i}", [128, 2 * H, 512], BF16) for i in range(2)]
            oinb = [Buf(f"oin{i}") for i in range(2)]
            sl_oi = [S.slot("oi0"), S.slot("oi1")]
            xr = [sb(p2, f"xr{i}", [128, D], F32) for i in range(2)]
            xrb = [Buf(f"xr{i}") for i in range(2)]
            sl_xr = [S.slot("xr0"), S.slot("xr1")]
            hh = [sb(p2, f"hh{i}", [128, D], F32) for i in range(2)]
            hhb = [Buf(f"hh{i}") for i in range(2)]
            yo = [sb(p2, f"yo{i}", [128, D], F32) for i in range(2)]
            yob = [Buf(f"yo{i}") for i in range(2)]
            sl_y = [S.slot("y0"), S.slot("y1")]
            j2 = sb(p2, "junk2", [128, D], BF16)
            j2b = Buf("junk2")
            st2 = sb(p2, "st2", [128, 2, 4], F32)
            st2b = [Buf("st2a"), Buf("st2b")]
            NCG = D // 512
            banks = PS_O + PS_S + PS_M
            bi_ = 0
            out_toks = []
            for g in range(NG):
                nbq = min(4, NOWN - 4 * g)
                gi = g % 2
                S.dma("sp", sl_oi[gi], oin[gi][:, :, 0:nbq * 128], oT_d[g, :, :, 0:nbq * 128], writes=[oinb[gi]])
                for blk in range(nbq):
                    ob = 4 * g + blk
                    k = ob % 2
                    S.dma("sp", sl_xr[k], xr[k][:, :], x_own[ob * 128:(ob + 1) * 128, :], writes=[xrb[k]])
                    for ng in range(NCG):
                        psi = banks[bi_ % 8]
                        bi_ += 1
                        for hd in range(2 * H):
                            mm(ps[psi][:, :], oin[gi][:, hd, blk * 128:(blk + 1) * 128], wout[:, hd, ng * 512:(ng + 1) * 512],
                               hd == 0, hd == 2 * H - 1, [oinb[gi], b_w2], [psb[psi]], signal=(hd == 2 * H - 1))
                        S.op("dve", lambda e: e.tensor_tensor(out=hh[k][:, ng * 512:(ng + 1) * 512], in0=ps[psi][:, :],
                                                              in1=xr[k][:, ng * 512:(ng + 1) * 512], op=ALU.add),
                             reads=[psb[psi], xrb[k]], writes=[hhb[k]])
                    sumsq(hh[k][:, :], j2[:, :], st2[:, k, 0:1], [hhb[k]], [j2b, st2b[k]])
                    rstd_from_ss(st2[:, k, 0:1], st2[:, k, 1:2], D, st2b[k])
                    S.op("dve", lambda e: e.scalar_tensor_tensor(out=yo[k][:, :], in0=hh[k][:, :], scalar=st2[:, k, 1:2],
                                                                 in1=fnw[:, :], op0=ALU.mult, op1=ALU.mult),
                         reads=[hhb[k], st2b[k], b_w2], writes=[yob[k]])
                    out_toks.append(S.dma("sp", sl_y[k], y_d[ob * 128:(ob + 1) * 128, :], yo[k][:, :], reads=[yob[k]]))
            for t in out_toks[-2:]:
                S._wait("sp", t)
            for t in out_toks[-2:]:
                S._wait("act", t)
    return nc


def _pc(a, C):
    n = a.shape[1]
    return np.ascontiguousarray(a.reshape(C, 128, n).transpose(1, 0, 2))


def make_in_maps(cfg, x, meta_tokens, attn_norm_w, w_in, diff_lambda, diff_subln_w,
                 mla_q_norm_w, w_uq, mla_kv_norm_w, w_ukv, w_out, final_norm_w):
    D, C, H, QL, KVL, S_ = cfg.D, cfg.C, cfg.H, cfg.QL, cfg.KVL, cfg.S
    NOWN, NKB, CQ, CKV = cfg.NOWN, cfg.NKB, cfg.CQ, cfg.CKV
    f32 = np.float32
    w_in = np.asarray(w_in, f32)[0]
    w_uq = np.asarray(w_uq, f32)[0]
    w_ukv = np.asarray(w_ukv, f32)[0]
    w_out = np.asarray(w_out, f32)[0]
    x = np.asarray(x, f32)
    AW = H * 128
    o_q, o_k, o_v, o_g = 0, AW, 2 * AW, 3 * AW
    o_cq = 4 * AW
    o_ckv = o_cq + QL
    o_kr = o_ckv + KVL
    o_bg = o_kr + 64
    wA = np.stack([_pc(np.concatenate([w_in[:, o_q + h * 128:o_q + (h + 1) * 128], w_in[:, o_k + h * 128:o_k + (h + 1) * 128],
                                       w_in[:, o_v + h * 128:o_v + (h + 1) * 128], w_in[:, o_g + h * 128:o_g + (h + 1) * 128]], axis=1), C)
                   for h in range(H)])
    wckvkr = _pc(w_in[:, o_ckv:o_ckv + KVL + 64], C)
    wcq = _pc(w_in[:, o_cq:o_cq + QL], C)
    wbg = np.stack([_pc(w_in[:, o_bg + h * 128:o_bg + (h + 1) * 128], C) for h in range(H)])
    perm = np.concatenate([np.arange(32, 64), np.arange(0, 32)])
    wuq = np.stack([_pc(np.concatenate([w_uq[:, h * 192:h * 192 + 128], w_uq[:, h * 192 + 128:h * 192 + 192],
                                        w_uq[:, h * 192 + 128:h * 192 + 192][:, perm]], axis=1), CQ) for h in range(H)])
    wukv = np.stack([_pc(w_ukv[:, h * 256:(h + 1) * 256], CKV) for h in range(H)])
    wout = _pc(w_out, 2 * H)
    meta_pad = np.zeros((128, D), f32)
    meta_pad[:16] = np.asarray(meta_tokens, f32)
    inv = (ROPE_THETA ** (-np.arange(32, dtype=np.float64) / 32.0))
    kpos = np.zeros((NKB, 128), np.float64)
    kpos[0] = np.arange(128)
    for kb in range(1, NKB):
        kpos[kb] = 16 + (kb - 1) * 128 + np.arange(128)
    angk = kpos[:, :, None] * inv[None, None, :]
    cosk = np.ascontiguousarray(np.cos(angk).transpose(1, 0, 2)).astype(f32)
    sink = np.ascontiguousarray(np.sin(angk).transpose(1, 0, 2)).astype(f32)
    ident = np.eye(128, dtype=f32)
    slopes = 2.0 ** (-8.0 * np.arange(1, H + 1) / 8.0)
    kl = np.arange(128, dtype=np.float64)
    biasmb = np.where(kl < 16, 0.0, NEG).astype(f32).reshape(128, 1)
    maps = []
    for core in range(cfg.ncores):
        b, r = core // cfg.RB, core % cfg.RB
        own_blocks = [4 * m + r for m in range(NOWN)]
        x_own = np.concatenate([x[b, i * 128:(i + 1) * 128] for i in own_blocks], axis=0)
        qpos = np.concatenate([16 + i * 128 + np.arange(128) for i in own_blocks]).astype(np.float64)
        angq = qpos[None, :] * inv[:, None]
        cosq = np.concatenate([np.cos(angq), np.cos(angq)], axis=0).astype(f32)
        sinq = np.concatenate([-np.sin(angq), np.sin(angq)], axis=0).astype(f32)
        NBX = 4 * NOWN + 4
        biasx = np.zeros((128, H, NBX), f32)
        biasm = np.zeros((128, H, NOWN), f32)
        for h in range(H):
            for ddx in range(NBX):
                dd = ddx - 3
                biasx[:, h, ddx] = -slopes[h] * (128.0 * (dd + r) + 127.0 - kl)
            for m in range(NOWN):
                v = -slopes[h] * (16.0 + 128.0 * (4 * m + r) + 127.0 - kl)
                biasm[:, h, m] = np.where(kl < 16, v, NEG)
        masks = np.zeros((128, 4, 128), f32)
        for jj in range(4):
            if jj == r:
                masks[:, jj, :] = np.where(kl[:, None] <= kl[None, :], 0.0, NEG)
            elif jj > r:
                masks[:, jj, :] = NEG
        maps.append({
            "x_all": np.ascontiguousarray(x[b]), "x_own": np.ascontiguousarray(x_own), "meta_pad": meta_pad,
            "wA": wA, "wckvkr": wckvkr, "wcq": wcq, "wbg": wbg, "wuq": wuq, "wukv": wukv, "wout": wout,
            "cosk": cosk, "sink": sink, "cosq": cosq, "sinq": sinq,
            "biasx": biasx, "biasm": biasm, "biasmb": biasmb, "masks": masks, "ident": ident,
            "attn_norm_w": np.asarray(attn_norm_w, f32).reshape(1, D),
            "q_norm_w": np.asarray(mla_q_norm_w, f32).reshape(1, QL),
            "kv_norm_w": np.asarray(mla_kv_norm_w, f32).reshape(1, KVL),
            "final_norm_w": np.asarray(final_norm_w, f32).reshape(1, D),
            "subln_w": np.asarray(diff_subln_w, f32).reshape(1, 128),
            "diff_lambda": np.asarray(diff_lambda, f32).reshape(1, 256),
        })
    return maps


def gather_out(cfg, results):
    out = np.zeros((cfg.B, cfg.S, cfg.D), np.float32)
    for core in range(cfg.ncores):
        b, r = core // cfg.RB, core % cfg.RB
        y = np.asarray(results[core]["y_own"], np.float32)
        for m in range(cfg.NOWN):
            i = 4 * m + r
            out[b, i * 128:(i + 1) * 128] = y[m * 128:(m + 1) * 128]
    return out


def kernel(x, meta_tokens, attn_norm_w, w_in, diff_lambda, diff_subln_w,
           mla_q_norm_w, w_uq, mla_kv_norm_w, w_ukv, w_out, final_norm_w):
    cfg = Cfg()
    nc = build_nc(cfg)
    in_maps = make_in_maps(cfg, x, meta_tokens, attn_norm_w, w_in, diff_lambda, diff_subln_w,
                           mla_q_norm_w, w_uq, mla_kv_norm_w, w_ukv, w_out, final_norm_w)
    res = run_bass_kernel_spmd(nc, in_maps, core_ids=list(range(cfg.ncores)))
    return gather_out(cfg, res.results)
```

```python
import contextlib
import numpy as np
import concourse.bass as bass
import concourse.mybir as mybir
from concourse.bass_utils import run_bass_kernel_spmd

F32 = mybir.dt.float32
BF16 = mybir.dt.bfloat16
ALU = mybir.AluOpType
AF = mybir.ActivationFunctionType

NEG = -30000.0
RMS_EPS = 1e-6
ROPE_THETA = 10000.0


class Cfg:
    def __init__(self, D=2048, S=8192, H=8, QL=512, KVL=256, B=2, slopes=None, nb=None, win_T=100.0):
        self.D, self.S, self.H, self.QL, self.KVL, self.B = D, S, H, QL, KVL, B
        self.slopes = list(slopes) if slopes is not None else [2.0 ** (-(h + 1)) for h in range(H)]
        self.nb = list(nb) if nb is not None else [4 if sl * 1663 <= 52 else (2 if sl * 639 <= 52 else 1) for sl in self.slopes]
        self.win_T = win_T
        self.C = D // 128
        self.NX = S // 128
        self.RB = 4
        self.NOWN = self.NX // self.RB
        self.NKB = self.NX + 1
        self.NG = (self.NOWN + 3) // 4
        self.NKSB = (self.NKB + 3) // 4
        self.CQ = QL // 128
        self.CKV = KVL // 128
        self.ncores = self.RB * B


class Tok:
    __slots__ = ("sem", "val", "key", "slot")

    def __init__(self, sem, val, key, slot=None):
        self.sem, self.val, self.key, self.slot = sem, val, key, slot


class Buf:
    __slots__ = ("name", "w", "r")

    def __init__(self, name):
        self.name, self.w, self.r = name, None, {}


class Sched:
    ROT = 30000

    def __init__(self, nc, es):
        self.nc, self.es = nc, es
        self.eng = dict(pe=nc.tensor, act=nc.scalar, dve=nc.vector, pool=nc.gpsimd, sp=nc.sync)
        self.prog = {}
        self.seen = {e: {} for e in self.eng}
        self.nsem = 0
        self.old = []
        for e in self.eng:
            self._new_prog(e)
        self.slots = []
        self.pending = {e: False for e in self.eng}

    def _sem(self, name):
        self.nsem += 1
        return self.es.enter_context(self.nc.semaphore(name))

    def _new_prog(self, e):
        if e in self.prog and self.prog[e][1] > 0:
            self.old.append(self.prog[e])
        k = f"p_{e}_{self.nsem}"
        self.prog[e] = [self._sem(k), 0, k]

    def slot(self, name):
        k = f"d_{name}_{self.nsem}"
        s = [self._sem(k), 0, k]
        self.slots.append(s)
        return s

    def _wait(self, e, tok):
        if tok is None:
            return
        val = tok.slot[1] if tok.slot is not None else tok.val
        if self.seen[e].get(tok.key, 0) >= val:
            return
        self.eng[e].wait_ge(tok.sem, val)
        self.seen[e][tok.key] = val

    def _deps(self, e, reads, writes):
        own = self.prog[e][2] if e == "pe" else None
        for b in reads:
            if b.w is not None and b.w.key != own:
                self._wait(e, b.w)
        for b in writes:
            if b.w is not None and b.w.key != own:
                self._wait(e, b.w)
            for t in b.r.values():
                if t.key != own:
                    self._wait(e, t)

    def _mark(self, tok, reads, writes):
        for b in writes:
            b.w, b.r = tok, {}
        for b in reads:
            b.r[tok.key] = tok

    def op(self, e, fn, reads=(), writes=(), signal=True):
        self._deps(e, reads, writes)
        ins = fn(self.eng[e])
        p = self.prog[e]
        if signal:
            p[1] += 1
            ins.then_inc(p[0], 1)
            tok = Tok(p[0], p[1], p[2])
            self.pending[e] = False
        else:
            tok = Tok(p[0], p[1] + 1, p[2])
            self.pending[e] = True
        self._mark(tok, reads, writes)
        if signal and p[1] >= self.ROT:
            self._new_prog(e)
        return tok

    def dma(self, q, slot, out, in_, reads=(), writes=()):
        self._deps(q, reads, writes)
        ins = self.eng[q].dma_start(out=out, in_=in_)
        slot[1] += 16
        ins.then_inc(slot[0], 16)
        tok = Tok(slot[0], slot[1], slot[2], slot)
        self._mark(tok, reads, writes)
        return tok

    def all_toks(self):
        toks = []
        for e, p in self.prog.items():
            assert not self.pending[e], f"engine {e} has unsignalled ops at barrier"
            if p[1] > 0:
                toks.append(Tok(p[0], p[1], p[2]))
        for s in self.slots:
            if s[1] > 0:
                toks.append(Tok(s[0], s[1], s[2]))
        return toks

    def barrier(self):
        toks = self.all_toks()
        for e in self.eng:
            for t in toks:
                self._wait(e, t)


def build_nc(cfg):
    D, C, H, QL, KVL = cfg.D, cfg.C, cfg.H, cfg.QL, cfg.KVL
    NOWN, NKB, NG, NKSB, CQ, CKV = cfg.NOWN, cfg.NKB, cfg.NG, cfg.NKSB, cfg.CQ, cfg.CKV
    NT = NKB * 128
    NOT = NOWN * 128
    NBX = 4 * NOWN + 4
    nc = bass.Bass("TRN2", target_bir_lowering=False)

    def din(name, shape, dt=F32):
        return nc.dram_tensor(name, list(shape), dt, kind="ExternalInput").ap()

    x_all = din("x_all", [cfg.S, D])
    x_own = din("x_own", [NOT, D])
    meta_pad = din("meta_pad", [128, D])
    wA_d = din("wA", [H, 128, C, 512])
    wckvkr_d = din("wckvkr", [128, C, KVL + 64])
    wcq_d = din("wcq", [128, C, QL])
    wbg_d = din("wbg", [H, 128, C, 128])
    wuq_d = din("wuq", [H, 128, CQ, 256])
    wukv_d = din("wukv", [H, 128, CKV, 256])
    wout_d = din("wout", [128, 2 * H, D])
    cosk_d = din("cosk", [128, NKB, 32])
    sink_d = din("sink", [128, NKB, 32])
    cosq_d = din("cosq", [64, NOT])
    sinq_d = din("sinq", [64, NOT])
    biasx_d = din("biasx", [128, H, NBX])
    biasm_d = din("biasm", [128, H, NOWN])
    biasmb_d = din("biasmb", [128, 1])
    masks_d = din("masks", [128, 4, 128])
    ident_d = din("ident", [128, 128])
    anw_d = din("attn_norm_w", [1, D])
    qnw_d = din("q_norm_w", [1, QL])
    kvnw_d = din("kv_norm_w", [1, KVL])
    fnw_d = din("final_norm_w", [1, D])
    subw_d = din("subln_w", [1, 128])
    lam_d = din("diff_lambda", [1, 256])
    y_d = nc.dram_tensor("y_own", [NOT, D], F32, kind="ExternalOutput").ap()
    hnT_all = nc.dram_tensor("hnT_all", [NKSB, 128, C, 512], BF16).ap()
    hnT_own = nc.dram_tensor("hnT_own", [NG, 128, C, 512], BF16).ap()
    oT_d = nc.dram_tensor("oT_scr", [NG, 128, 2 * H, 512], BF16).ap()

    es = contextlib.ExitStack()
    with es:
        S = Sched(nc, es)
        ctr = {"misc": 0, "s": 0, "pt": 0}

        def sb(stack, name, shape, dt):
            ctr["sb"] = ctr.get("sb", 0) + 1
            return stack.enter_context(nc.sbuf_tensor(f"s{ctr['sb']}_{name}", list(shape), dt))

        def bc(ap_1xn):
            return ap_1xn.partition_broadcast(128)

        ps = [es.enter_context(nc.psum_tensor(f"ps{i}", [128, 512], F32)) for i in range(8)]
        psb = [Buf(f"ps{i}") for i in range(8)]
        PS_S = [0, 1]
        PS_O = [2, 3, 4, 5]
        PS_M = [6, 7]

        def misc():
            i = PS_M[ctr["misc"] % 2]
            ctr["misc"] += 1
            return i

        ident = sb(es, "ident", [128, 128], BF16)
        masks = sb(es, "masks", [128, 4, 128], BF16)
        cst32 = sb(es, "cst32", [128, 640], F32)
        lamt = sb(es, "lamt", [128, 256], F32)
        lamj = sb(es, "lamj", [128, 64], F32)
        lams = sb(es, "lams", [128, 8], F32)
        subw = sb(es, "subw", [128, 128], F32)
        biasx = sb(es, "biasx", [128, H, NBX], F32)
        biasm = sb(es, "biasm", [128, H, NOWN], F32)
        biasmb = sb(es, "biasmb", [128, 1], F32)
        epsb = sb(es, "epsb", [128, 1], F32)
        b_const = Buf("const")
        sl_c = S.slot("const")
        S.dma("sp", sl_c, cst32[:, 0:512], masks_d.rearrange("p a b -> p (a b)"), writes=[b_const])
        S.dma("sp", sl_c, cst32[:, 512:640], ident_d[:, :], writes=[b_const])
        S.dma("sp", sl_c, lamt[:, :], bc(lam_d), writes=[b_const])
        S.dma("sp", sl_c, subw[:, :], bc(subw_d), writes=[b_const])
        S.dma("sp", sl_c, biasx[:, :, :], biasx_d[:, :, :], writes=[b_const])
        S.dma("sp", sl_c, biasm[:, :, :], biasm_d[:, :, :], writes=[b_const])
        S.dma("sp", sl_c, biasmb[:, :], biasmb_d[:, :], writes=[b_const])
        b_c2 = Buf("const2")
        S.op("pool", lambda e: e.memset(epsb[:, :], RMS_EPS), writes=[b_c2])
        S.op("dve", lambda e: e.tensor_copy(out=masks[:, :, :].rearrange("p a b -> p (a b)"), in_=cst32[:, 0:512]),
             reads=[b_const], writes=[b_c2])
        S.op("dve", lambda e: e.tensor_copy(out=ident[:, :], in_=cst32[:, 512:640]), reads=[b_const, b_c2], writes=[b_c2])
        b_lam = Buf("lam")
        S.op("dve", lambda e: e.scalar_tensor_tensor(out=lamj[:, :], in0=lamt[:, 0:64], scalar=1.0, in1=lamt[:, 64:128],
                                                     op0=ALU.mult, op1=ALU.mult, accum_out=lams[:, 0:1]),
             reads=[b_const], writes=[b_lam])
        S.op("dve", lambda e: e.scalar_tensor_tensor(out=lamj[:, :], in0=lamt[:, 128:192], scalar=1.0, in1=lamt[:, 192:256],
                                                     op0=ALU.mult, op1=ALU.mult, accum_out=lams[:, 1:2]),
             reads=[b_const, b_lam], writes=[b_lam])
        S.op("act", lambda e: e.activation(out=lams[:, 2:4], in_=lams[:, 0:2], func=AF.Exp), reads=[b_lam], writes=[b_lam])
        S.op("dve", lambda e: e.tensor_tensor(out=lams[:, 4:5], in0=lams[:, 2:3], in1=lams[:, 3:4], op=ALU.subtract),
             reads=[b_lam], writes=[b_lam])
        S.op("dve", lambda e: e.tensor_scalar(out=lams[:, 5:6], in0=lams[:, 4:5], scalar1=0.2, scalar2=-1.0,
                                              op0=ALU.add, op1=ALU.mult), reads=[b_lam], writes=[b_lam])
        S.op("dve", lambda e: e.tensor_scalar(out=subw[:, :], in0=subw[:, :], scalar1=0.8, scalar2=None, op0=ALU.mult),
             reads=[b_const, b_c2], writes=[b_c2])

        def rstd_from_ss(ss_ap, out_ap, n, buf):
            S.op("act", lambda e: e.activation(out=out_ap, in_=ss_ap, func=AF.Ln, scale=1.0 / n, bias=epsb[:, 0:1]),
                 reads=[buf, b_c2], writes=[buf])
            S.op("act", lambda e: e.activation(out=out_ap, in_=out_ap, func=AF.Exp, scale=-0.5),
                 reads=[buf], writes=[buf])

        def sumsq(src_ap, junk_ap, acc_ap, rd, wr):
            S.op("dve", lambda e: e.scalar_tensor_tensor(out=junk_ap, in0=src_ap, scalar=1.0, in1=src_ap,
                                                         op0=ALU.mult, op1=ALU.mult, accum_out=acc_ap),
                 reads=rd, writes=wr)

        def transpose_to(src_ap, K, M, psi, col0, rd, signal):
            pv = ps[psi][:, :].bitcast(BF16)
            S.op("pe", lambda e: e.transpose(out=pv[0:M, col0:col0 + K], in_=src_ap, identity=ident[0:K, 0:K]),
                 reads=rd + [b_c2], writes=[psb[psi]], signal=signal)

        def mm(out, lhsT, rhs, start, stop, rd, wr, signal):
            S.op("pe", lambda e: e.matmul(out, lhsT=lhsT, rhs=rhs, start=start, stop=stop),
                 reads=rd, writes=wr, signal=signal)

        def load_w(stack, name, shape, src_ap, slot, buf):
            t = sb(stack, name, shape, BF16)
            S.dma("pool", slot, t[:], src_ap, writes=[buf])
            return t

        with contextlib.ExitStack() as p0:
            anw = sb(p0, "anw", [128, D], F32)
            b_anw = Buf("anw")
            sl_anw = S.slot("anw")
            S.dma("sp", sl_anw, anw[:, :], bc(anw_d), writes=[b_anw])
            xt = [sb(p0, f"xt{i}", [128, D], F32) for i in range(2)]
            xb = [Buf(f"xt{i}") for i in range(2)]
            junk = sb(p0, "junk0", [128, D], BF16)
            jb = Buf("junk0")
            hn = [sb(p0, f"hn{i}", [128, D], BF16) for i in range(2)]
            hb = [Buf(f"hn{i}") for i in range(2)]
            st = sb(p0, "st0", [128, 8], F32)
            stb = [Buf("st0a"), Buf("st0b")]
            hT = [sb(p0, f"hT{i}", [128, C, 512], BF16) for i in range(2)]
            hTb = [Buf(f"hT{i}") for i in range(2)]
            sl_x = [S.slot("x0"), S.slot("x1")]
            sl_h = [S.slot("h0"), S.slot("h1")]
            blocks = [("k", kb) for kb in range(NKB)] + [("o", ob) for ob in range(NOWN)]
            nsb, cnt, cur = 0, 0, 0
            for kind, bi in blocks:
                i2 = cnt % 2
                sbi, bl = bi // 4, bi % 4
                if kind == "k":
                    src = meta_pad[:, :] if bi == 0 else x_all[(bi - 1) * 128: bi * 128, :]
                    nblk = min(4, NKB - sbi * 4)
                    dst = hnT_all
                else:
                    src = x_own[bi * 128:(bi + 1) * 128, :]
                    nblk = min(4, NOWN - sbi * 4)
                    dst = hnT_own
                if bl == 0:
                    cur = nsb % 2
                    nsb += 1
                S.dma("sp", sl_x[i2], xt[i2][:, :], src, writes=[xb[i2]])
                sumsq(xt[i2][:, :], junk[:, :], st[:, i2:i2 + 1], [xb[i2]], [jb, stb[i2]])
                rstd_from_ss(st[:, i2:i2 + 1], st[:, 2 + i2:3 + i2], D, stb[i2])
                S.op("dve", lambda e: e.scalar_tensor_tensor(out=hn[i2][:, :], in0=xt[i2][:, :], scalar=st[:, 2 + i2:3 + i2],
                                                             in1=anw[:, :], op0=ALU.mult, op1=ALU.mult),
                     reads=[xb[i2], stb[i2], b_anw], writes=[hb[i2]])
                for c0 in range(0, C, 8):
                    ncz = min(8, C - c0)
                    psi = misc()
                    for c in range(c0, c0 + ncz):
                        transpose_to(hn[i2][:, c * 128:(c + 1) * 128], 128, 128, psi, (c - c0) * 128, [hb[i2]],
                                     signal=(c == c0 + ncz - 1))
                    pv = ps[psi][:, :].bitcast(BF16)
                    src_v = pv[:, 0:ncz * 128].rearrange("p (c t) -> p c t", c=ncz)
                    dst_v = hT[cur][:, c0:c0 + ncz, bl * 128:(bl + 1) * 128]
                    S.op("dve", lambda e: e.tensor_copy(out=dst_v, in_=src_v), reads=[psb[psi]], writes=[hTb[cur]])
                if bl == nblk - 1:
                    S.dma("sp", sl_h[cur], dst[sbi, :, :, 0:nblk * 128], hT[cur][:, :, 0:nblk * 128], reads=[hTb[cur]])
                cnt += 1
        S.barrier()

        with contextlib.ExitStack() as pa:
            vaug = sb(pa, "vaug", [128, NKB, 130], BF16)
            vb = [Buf(f"v{i}") for i in range(NKSB)]
            b_ones = Buf("ones")
            S.op("pool", lambda e: e.memset(vaug[:, :, 128:130], 1.0), writes=[b_ones])
            kT = sb(pa, "kT", [128, NT], BF16)
            kb_ = [Buf(f"k{i}") for i in range(NKSB)]
            hs = [sb(pa, f"hs{i}", [128, C, 512], BF16) for i in range(2)]
            hsb = [Buf(f"hs{i}") for i in range(2)]
            sl_hs = [S.slot("hs0"), S.slot("hs1")]
            ctr["hs"] = 0
            pT = [sb(pa, f"pT{i}", [128, 512], BF16) for i in range(3)]
            pTb = [Buf(f"pT{i}") for i in range(3)]
            qT = sb(pa, "qT", [128, 512], BF16)
            qTb = Buf("qT")
            gsil = sb(pa, "gsil", [128, 4, 128], F32)
            gsb = Buf("gsil")
            gtmp = sb(pa, "gtmp", [128, 512], F32)
            gtb = Buf("gtmp")
            t0 = sb(pa, "t0", [128, 4, 128], F32)
            t0b = [Buf(f"t0_{i}") for i in range(4)]
            dd_ = sb(pa, "dd", [128, 2, 128], F32)
            ddb = [Buf("dd0"), Buf("dd1")]
            ejunk = sb(pa, "ejunk", [128, 128], F32)
            ejb = Buf("ejunk")
            est = sb(pa, "est", [128, 2, 8], F32)
            estb = [Buf("est0"), Buf("est1")]
            obf = sb(pa, "obf", [128, 2, 128], BF16)
            obb = [Buf("obf0"), Buf("obf1")]
            oTs = [sb(pa, f"oTs{i}", [128, 512], BF16) for i in range(2)]
            oTb = [Buf(f"oTs{i}") for i in range(2)]
            sl_o = [S.slot("o0"), S.slot("o1")]
            ctr["ep"] = 0
            ctr["ot"] = 0
            sl_w = S.slot("w")

            def stream_hs(src_ap, ncols):
                i = ctr["hs"] % 2
                ctr["hs"] += 1
                S.dma("sp", sl_hs[i], hs[i][:, :, 0:ncols], src_ap, writes=[hsb[i]])
                return i

            def attention_pass(g, parts, scale, bias_fn, nb, part_fn, epilogue):
                nbq = min(4, NOWN - 4 * g)
                kbs_of = [[kb for kb in range(0, 4 * (4 * g + qb) + 3 + 2) if part_fn(4 * g + qb, kb)] for qb in range(nbq)]
                all_kb = sorted(set().union(*[set(l) for l in kbs_of]))
                for kb in all_kb:
                    j = kb - 1
                    qbs = [qb for qb in range(nbq) if kb in kbs_of[qb]]
                    qs, qe = qbs[0], qbs[-1] + 1
                    assert qbs == list(range(qs, qe))
                    mq = (j // 4 - 4 * g) if kb > 0 else -1
                    has_mask = mq >= 0
                    assert (not has_mask) or (mq == qs)
                    si = PS_S[ctr["s"] % 2]
                    ctr["s"] += 1
                    pi = ctr["pt"] % 3
                    ctr["pt"] += 1
                    c0, c1 = qs * 128, qe * 128
                    for ip, (kfn, qfn, rdb) in enumerate(parts):
                        last = (ip == len(parts) - 1) and not has_mask
                        mm(ps[si][:, c0:c1], kfn(kb), qfn(c0, c1), ip == 0, last,
                           rdb + [kb_[kb // 4], qTb], [psb[si]], signal=last)
                    if has_mask:
                        mm(ps[si][:, c0:c0 + 128], ident[:, :], masks[:, j % 4, :], False, True,
                           [b_c2], [psb[si]], signal=True)
                    for r0 in range(0, nbq, nb):
                        lo, hi = max(r0, qs), min(r0 + nb, qe)
                        if lo >= hi:
                            continue
                        S.op("act", lambda e: e.activation(out=pT[pi][:, lo * 128:hi * 128], in_=ps[si][:, lo * 128:hi * 128],
                                                           func=AF.Exp, bias=bias_fn(kb, 4 * g + hi - 1), scale=scale),
                             reads=[psb[si], b_const], writes=[pTb[pi]])
                    for qb in range(qs, qe):
                        mm(ps[PS_O[qb]][:, 0:129], pT[pi][:, qb * 128:(qb + 1) * 128], vaug[:, kb, 0:129],
                           kb == kbs_of[qb][0], kb == kbs_of[qb][-1], [pTb[pi], vb[kb // 4], b_ones], [psb[PS_O[qb]]],
                           signal=(qb == qe - 1))
                    for qb in range(qs, qe):
                        if kb == kbs_of[qb][-1]:
                            epilogue(qb)

            def finish_o(g, qb, head_idx, o_src_fn):
                nbq = min(4, NOWN - 4 * g)
                k = ctr["ep"] % 2
                ctr["ep"] += 1
                o_src_fn(obf[:, k, :], obb[k])
                psi = misc()
                transpose_to(obf[:, k, :], 128, 128, psi, 0, [obb[k]], signal=True)
                oi = ctr["ot"] % 2
                pv = ps[psi][:, :].bitcast(BF16)
                S.op("dve", lambda e: e.tensor_copy(out=oTs[oi][:, qb * 128:(qb + 1) * 128], in_=pv[:, 0:128]),
                     reads=[psb[psi]], writes=[oTb[oi]])
                if qb == nbq - 1:
                    S.dma("sp", sl_o[oi], oT_d[g, :, head_idx, 0:nbq * 128], oTs[oi][:, 0:nbq * 128], reads=[oTb[oi]])
                    ctr["ot"] += 1

            def gate_proj(hi_buf, nbq, w_ap_fn):
                psi = misc()
                for blk in range(nbq):
                    for c in range(C):
                        mm(ps[psi][:, blk * 128:(blk + 1) * 128], hs[hi_buf][:, c, blk * 128:(blk + 1) * 128], w_ap_fn(c),
                           c == 0, c == C - 1, [hsb[hi_buf], b_w], [psb[psi]], signal=(c == C - 1 and blk == nbq - 1))
                n = nbq * 128
                S.op("act", lambda e: e.activation(out=gtmp[:, 0:n], in_=ps[psi][:, 0:n], func=AF.Exp, scale=-1.0),
                     reads=[psb[psi]], writes=[gtb])
                S.op("dve", lambda e: e.tensor_scalar(out=gtmp[:, 0:n], in0=gtmp[:, 0:n], scalar1=1.0, scalar2=None, op0=ALU.add),
                     reads=[gtb], writes=[gtb])
                S.op("dve", lambda e: e.reciprocal(out=gtmp[:, 0:n], in_=gtmp[:, 0:n]), reads=[gtb], writes=[gtb])
                return psi

            b_w = Buf("w")
            for h in range(H):
                with contextlib.ExitStack() as ph:
                    wA = load_w(ph, "wA_sb", [128, C, 512], wA_d[h], sl_w, b_w)
                    for ksb in range(NKSB):
                        nblk = min(4, NKB - 4 * ksb)
                        n = nblk * 128
                        hi_ = stream_hs(hnT_all[ksb, :, :, 0:n], n)
                        psi = misc()
                        for c in range(C):
                            mm(ps[psi][:, 0:n], wA[:, c, 128:256], hs[hi_][:, c, 0:n], c == 0, c == C - 1,
                               [hsb[hi_], b_w], [psb[psi]], signal=(c == C - 1))
                        S.op("dve", lambda e: e.tensor_copy(out=kT[:, ksb * 512:ksb * 512 + n], in_=ps[psi][:, 0:n]),
                             reads=[psb[psi]], writes=[kb_[ksb]])
                        psi = misc()
                        for blk in range(nblk):
                            for c in range(C):
                                mm(ps[psi][:, blk * 128:(blk + 1) * 128], hs[hi_][:, c, blk * 128:(blk + 1) * 128],
                                   wA[:, c, 256:384], c == 0, c == C - 1, [hsb[hi_], b_w], [psb[psi]],
                                   signal=(c == C - 1 and blk == nblk - 1))
                        S.op("dve", lambda e: e.tensor_copy(out=vaug[:, 4 * ksb:4 * ksb + nblk, 0:128],
                                                            in_=ps[psi][:, 0:n].rearrange("p (b d) -> p b d", b=nblk)),
                             reads=[psb[psi]], writes=[vb[ksb]])
                    for g in range(NG):
                        nbq = min(4, NOWN - 4 * g)
                        n = nbq * 128
                        hi_ = stream_hs(hnT_own[g, :, :, 0:n], n)
                        psi = misc()
                        for c in range(C):
                            mm(ps[psi][:, 0:n], wA[:, c, 0:128], hs[hi_][:, c, 0:n], c == 0, c == C - 1,
                               [hsb[hi_], b_w], [psb[psi]], signal=(c == C - 1))
                        S.op("dve", lambda e: e.tensor_copy(out=qT[:, 0:n], in_=ps[psi][:, 0:n]), reads=[psb[psi]], writes=[qTb])
                        psg = gate_proj(hi_, nbq, lambda c: wA[:, c, 384:512])
                        for blk in range(nbq):
                            S.op("dve", lambda e: e.tensor_tensor(out=gsil[:, blk, :], in0=ps[psg][:, blk * 128:(blk + 1) * 128],
                                                                  in1=gtmp[:, blk * 128:(blk + 1) * 128], op=ALU.mult),
                                 reads=[psb[psg], gtb], writes=[gsb])
                            S.op("dve", lambda e: e.tensor_tensor(out=gsil[:, blk, :], in0=gsil[:, blk, :], in1=subw[:, :], op=ALU.mult),
                                 reads=[gsb, b_c2], writes=[gsb])

                        def bias_fn(kb, m, h=h):
                            if kb == 0:
                                return biasm[:, h, m:m + 1]
                            ddx = 4 * m - (kb - 1) + 3
                            return biasx[:, h, ddx:ddx + 1]

                        def part_a(m, kb, h=h):
                            sl = cfg.slopes[h]
                            if kb == 0:
                                return sl * (512.0 * m + 1.0) <= cfg.win_T
                            jj = kb - 1
                            return jj <= 4 * m + 3 and sl * (128.0 * (4 * m - jj) - 127.0) <= cfg.win_T

                        def epi0(qb):
                            k = ctr["ep"] % 2
                            acc = ps[PS_O[qb]]
                            S.op("dve", lambda e: e.reciprocal(out=est[:, k, 0:1], in_=acc[:, 128:129]),
                                 reads=[psb[PS_O[qb]]], writes=[estb[k]])
                            S.op("dve", lambda e: e.tensor_scalar(out=t0[:, qb, :], in0=acc[:, 0:128], scalar1=est[:, k, 0:1],
                                                                  scalar2=None, op0=ALU.mult),
                                 reads=[psb[PS_O[qb]], estb[k]], writes=[t0b[qb]])
                            ctr["ep"] += 1

                        def epi1(qb, h=h, g=g):
                            k = ctr["ep"] % 2
                            acc = ps[PS_O[qb]]
                            S.op("dve", lambda e: e.reciprocal(out=est[:, k, 0:1], in_=acc[:, 128:129]),
                                 reads=[psb[PS_O[qb]]], writes=[estb[k]])
                            S.op("dve", lambda e: e.tensor_tensor(out=est[:, k, 1:2], in0=est[:, k, 0:1], in1=lams[:, 5:6], op=ALU.mult),
                                 reads=[estb[k], b_lam], writes=[estb[k]])
                            S.op("dve", lambda e: e.scalar_tensor_tensor(out=dd_[:, k, :], in0=acc[:, 0:128], scalar=est[:, k, 1:2],
                                                                         in1=t0[:, qb, :], op0=ALU.mult, op1=ALU.add),
                                 reads=[psb[PS_O[qb]], estb[k], t0b[qb]], writes=[ddb[k]])
                            S.op("dve", lambda e: e.scalar_tensor_tensor(out=ejunk[:, :], in0=dd_[:, k, :], scalar=1.0, in1=dd_[:, k, :],
                                                                         op0=ALU.mult, op1=ALU.mult, accum_out=est[:, k, 2:3]),
                                 reads=[ddb[k], estb[k]], writes=[ejb, estb[k]])
                            rstd_from_ss(est[:, k, 2:3], est[:, k, 3:4], 128, estb[k])

                            def mk(out_ap, out_buf):
                                S.op("dve", lambda e: e.scalar_tensor_tensor(out=out_ap, in0=dd_[:, k, :], scalar=est[:, k, 3:4],
                                                                             in1=gsil[:, qb, :], op0=ALU.mult, op1=ALU.mult),
                                     reads=[ddb[k], estb[k], gsb], writes=[out_buf])
                            finish_o(g, qb, h, mk)

                        for comp in range(2):
                            r0 = 64 * comp
                            attention_pass(
                                g,
                                [(lambda kb, r0=r0: kT[r0:r0 + 64, kb * 128:(kb + 1) * 128],
                                  lambda c0, c1, r0=r0: qT[r0:r0 + 64, c0:c1], [])],
                                0.125, bias_fn, cfg.nb[h], part_a, epi0 if comp == 0 else epi1)
            S.barrier()

            with contextlib.ExitStack() as pb:
                ckvT = sb(pb, "ckvT", [128, CKV, NT], BF16)
                kpeT = sb(pb, "kpeT", [64, NT], BF16)
                cqT = sb(pb, "cqT", [128, CQ, NOT], BF16)
                b_ckv = [Buf(f"ckv{i}") for i in range(NKSB)]
                b_cq = [Buf(f"cq{i}") for i in range(NG)]
                b_tab = Buf("tabs")
                sl_t = S.slot("tabs")
                with contextlib.ExitStack() as pp:
                    wck = load_w(pp, "wck", [128, C, KVL + 64], wckvkr_d[:, :, :], sl_w, b_w)
                    wcq = load_w(pp, "wcq", [128, C, QL], wcq_d[:, :, :], sl_w, b_w)
                    cosk = sb(pp, "cosk", [128, NKB, 32], F32)
                    sink = sb(pp, "sink", [128, NKB, 32], F32)
                    kvnw = sb(pp, "kvnw", [128, KVL], F32)
                    qnw = sb(pp, "qnw", [128, QL], F32)
                    S.dma("sp", sl_t, cosk[:, :, :], cosk_d[:, :, :], writes=[b_tab])
                    S.dma("sp", sl_t, sink[:, :, :], sink_d[:, :, :], writes=[b_tab])
                    S.dma("sp", sl_t, kvnw[:, :], bc(kvnw_d), writes=[b_tab])
                    S.dma("sp", sl_t, qnw[:, :], bc(qnw_d), writes=[b_tab])
                    pj = sb(pp, "pj", [128, 512], BF16)
                    pjb = Buf("pj")
                    pf = sb(pp, "pf", [128, 512], F32)
                    pfb = Buf("pf")
                    pst = sb(pp, "pst", [128, 2, 4], F32)
                    pstb = [Buf("pst0"), Buf("pst1")]
                    cn = sb(pp, "cn", [128, 2, 512], BF16)
                    cnb = [Buf("cn0"), Buf("cn1")]
                    rt = sb(pp, "rt", [128, 4, 32], F32)
                    rtb = Buf("rt")
                    k2 = 0
                    for ksb in range(NKSB):
                        nblk = min(4, NKB - 4 * ksb)
                        hi_ = stream_hs(hnT_all[ksb, :, :, 0:nblk * 128], nblk * 128)
                        for blk in range(nblk):
                            kb = 4 * ksb + blk
                            k = k2 % 2
                            k2 += 1
                            psi = misc()
                            for c in range(C):
                                mm(ps[psi][:, 0:KVL + 64], hs[hi_][:, c, blk * 128:(blk + 1) * 128], wck[:, c, :],
                                   c == 0, c == C - 1, [hsb[hi_], b_w], [psb[psi]], signal=(c == C - 1))
                            S.op("dve", lambda e: e.tensor_copy(out=pf[:, 0:KVL], in_=ps[psi][:, 0:KVL]), reads=[psb[psi]], writes=[pfb])
                            sumsq(pf[:, 0:KVL], pj[:, 0:KVL], pst[:, k, 0:1], [pfb], [pjb, pstb[k]])
                            rstd_from_ss(pst[:, k, 0:1], pst[:, k, 1:2], KVL, pstb[k])
                            S.op("dve", lambda e: e.scalar_tensor_tensor(out=cn[:, k, 0:KVL], in0=ps[psi][:, 0:KVL], scalar=pst[:, k, 1:2],
                                                                         in1=kvnw[:, :], op0=ALU.mult, op1=ALU.mult),
                                 reads=[psb[psi], pstb[k], b_tab], writes=[cnb[k]])
                            x1 = ps[psi][:, KVL:KVL + 32]
                            x2 = ps[psi][:, KVL + 32:KVL + 64]
                            S.op("dve", lambda e: e.tensor_tensor(out=rt[:, 0, :], in0=x1, in1=cosk[:, kb, :], op=ALU.mult),
                                 reads=[psb[psi], b_tab], writes=[rtb])
                            S.op("dve", lambda e: e.tensor_tensor(out=rt[:, 1, :], in0=x2, in1=sink[:, kb, :], op=ALU.mult),
                                 reads=[psb[psi], b_tab, rtb], writes=[rtb])
                            S.op("dve", lambda e: e.tensor_tensor(out=rt[:, 2, :], in0=x1, in1=sink[:, kb, :], op=ALU.mult),
                                 reads=[psb[psi], b_tab, rtb], writes=[rtb])
                            S.op("dve", lambda e: e.tensor_tensor(out=rt[:, 3, :], in0=x2, in1=cosk[:, kb, :], op=ALU.mult),
                                 reads=[psb[psi], b_tab, rtb], writes=[rtb])
                            S.op("dve", lambda e: e.tensor_tensor(out=cn[:, k, KVL:KVL + 32], in0=rt[:, 0, :], in1=rt[:, 1, :], op=ALU.subtract),
                                 reads=[rtb, cnb[k]], writes=[cnb[k]])
                            S.op("dve", lambda e: e.tensor_tensor(out=cn[:, k, KVL + 32:KVL + 64], in0=rt[:, 2, :], in1=rt[:, 3, :], op=ALU.add),
                                 reads=[rtb, cnb[k]], writes=[cnb[k]])
                            pst_ = misc()
                            for c in range(CKV):
                                transpose_to(cn[:, k, c * 128:(c + 1) * 128], 128, 128, pst_, c * 128, [cnb[k]], signal=False)
                            transpose_to(cn[:, k, KVL:KVL + 64], 128, 64, pst_, CKV * 128, [cnb[k]], signal=True)
                            pv = ps[pst_][:, :].bitcast(BF16)
                            S.op("dve", lambda e: e.tensor_copy(out=ckvT[:, :, kb * 128:(kb + 1) * 128],
                                                                in_=pv[:, 0:CKV * 128].rearrange("p (c t) -> p c t", c=CKV)),
                                 reads=[psb[pst_]], writes=[b_ckv[ksb]])
                            S.op("dve", lambda e: e.tensor_copy(out=kpeT[0:64, kb * 128:(kb + 1) * 128],
                                                                in_=pv[0:64, CKV * 128:CKV * 128 + 128]),
                                 reads=[psb[pst_]], writes=[b_ckv[ksb]])
                    for g in range(NG):
                        nbq = min(4, NOWN - 4 * g)
                        hi_ = stream_hs(hnT_own[g, :, :, 0:nbq * 128], nbq * 128)
                        for blk in range(nbq):
                            ob = 4 * g + blk
                            k = k2 % 2
                            k2 += 1
                            psi = misc()
                            for c in range(C):
                                mm(ps[psi][:, 0:QL], hs[hi_][:, c, blk * 128:(blk + 1) * 128], wcq[:, c, :],
                                   c == 0, c == C - 1, [hsb[hi_], b_w], [psb[psi]], signal=(c == C - 1))
                            S.op("dve", lambda e: e.tensor_copy(out=pf[:, 0:QL], in_=ps[psi][:, 0:QL]), reads=[psb[psi]], writes=[pfb])
                            sumsq(pf[:, 0:QL], pj[:, 0:QL], pst[:, k, 0:1], [pfb], [pjb, pstb[k]])
                            rstd_from_ss(pst[:, k, 0:1], pst[:, k, 1:2], QL, pstb[k])
                            S.op("dve", lambda e: e.scalar_tensor_tensor(out=cn[:, k, 0:QL], in0=ps[psi][:, 0:QL], scalar=pst[:, k, 1:2],
                                                                         in1=qnw[:, :], op0=ALU.mult, op1=ALU.mult),
                                 reads=[psb[psi], pstb[k], b_tab], writes=[cnb[k]])
                            pst_ = misc()
                            for c in range(CQ):
                                transpose_to(cn[:, k, c * 128:(c + 1) * 128], 128, 128, pst_, c * 128, [cnb[k]], signal=(c == CQ - 1))
                            pv = ps[pst_][:, :].bitcast(BF16)
                            S.op("dve", lambda e: e.tensor_copy(out=cqT[:, :, ob * 128:(ob + 1) * 128],
                                                                in_=pv[:, 0:CQ * 128].rearrange("p (c t) -> p c t", c=CQ)),
                                 reads=[psb[pst_]], writes=[b_cq[g]])
                S.barrier()

                cosq = sb(pb, "cosq", [64, NOT], F32)
                sinq = sb(pb, "sinq", [64, NOT], F32)
                S.dma("sp", sl_t, cosq[:, :], cosq_d[:, :], writes=[b_tab])
                S.dma("sp", sl_t, sinq[:, :], sinq_d[:, :], writes=[b_tab])
                qpe = sb(pb, "qpe", [64, 512], BF16)
                qpeb = Buf("qpe")
                qr = sb(pb, "qr", [64, 2, 512], F32)
                qrb = Buf("qr")
                for h in range(H):
                    with contextlib.ExitStack() as ph:
                        wbg = load_w(ph, "wbg_sb", [128, C, 128], wbg_d[h], sl_w, b_w)
                        wuq = load_w(ph, "wuq_sb", [128, CQ, 256], wuq_d[h], sl_w, b_w)
                        wukv = load_w(ph, "wukv_sb", [128, CKV, 256], wukv_d[h], sl_w, b_w)
                        for ksb in range(NKSB):
                            nblk = min(4, NKB - 4 * ksb)
                            n = nblk * 128
                            t0_ = ksb * 512
                            psi = misc()
                            for c in range(CKV):
                                mm(ps[psi][:, 0:n], wukv[:, c, 0:128], ckvT[:, c, t0_:t0_ + n], c == 0, c == CKV - 1,
                                   [b_ckv[ksb], b_w], [psb[psi]], signal=(c == CKV - 1))
                            S.op("dve", lambda e: e.tensor_copy(out=kT[:, t0_:t0_ + n], in_=ps[psi][:, 0:n]),
                                 reads=[psb[psi]], writes=[kb_[ksb]])
                            psi = misc()
                            for blk in range(nblk):
                                for c in range(CKV):
                                    mm(ps[psi][:, blk * 128:(blk + 1) * 128], ckvT[:, c, t0_ + blk * 128:t0_ + (blk + 1) * 128],
                                       wukv[:, c, 128:256], c == 0, c == CKV - 1, [b_ckv[ksb], b_w], [psb[psi]],
                                       signal=(c == CKV - 1 and blk == nblk - 1))
                            S.op("dve", lambda e: e.tensor_copy(out=vaug[:, 4 * ksb:4 * ksb + nblk, 0:128],
                                                                in_=ps[psi][:, 0:n].rearrange("p (b d) -> p b d", b=nblk)),
                                 reads=[psb[psi]], writes=[vb[ksb]])
                        for g in range(NG):
                            nbq = min(4, NOWN - 4 * g)
                            n = nbq * 128
                            o0 = g * 512
                            hi_ = stream_hs(hnT_own[g, :, :, 0:n], n)
                            psg = gate_proj(hi_, nbq, lambda c: wbg[:, c, :])
                            S.op("dve", lambda e: e.tensor_tensor(out=gsil[:, 0:nbq, :].rearrange("p b d -> p (b d)"),
                                                                  in0=ps[psg][:, 0:n], in1=gtmp[:, 0:n], op=ALU.mult),
                                 reads=[psb[psg], gtb], writes=[gsb])
                            psi = misc()
                            for c in range(CQ):
                                mm(ps[psi][:, 0:n], wuq[:, c, 0:128], cqT[:, c, o0:o0 + n], c == 0, c == CQ - 1,
                                   [b_cq[g], b_w], [psb[psi]], signal=(c == CQ - 1))
                            S.op("dve", lambda e: e.tensor_copy(out=qT[:, 0:n], in_=ps[psi][:, 0:n]), reads=[psb[psi]], writes=[qTb])
                            psi = misc()
                            for c in range(CQ):
                                mm(ps[psi][0:64, 0:n], wuq[:, c, 128:192], cqT[:, c, o0:o0 + n], c == 0, c == CQ - 1,
                                   [b_cq[g], b_w], [psb[psi]], signal=(c == CQ - 1))
                            S.op("dve", lambda e: e.tensor_tensor(out=qr[:, 0, 0:n], in0=ps[psi][0:64, 0:n], in1=cosq[:, o0:o0 + n], op=ALU.mult),
                                 reads=[psb[psi], b_tab], writes=[qrb])
                            psi = misc()
                            for c in range(CQ):
                                mm(ps[psi][0:64, 0:n], wuq[:, c, 192:256], cqT[:, c, o0:o0 + n], c == 0, c == CQ - 1,
                                   [b_cq[g], b_w], [psb[psi]], signal=(c == CQ - 1))
                            S.op("dve", lambda e: e.tensor_tensor(out=qr[:, 1, 0:n], in0=ps[psi][0:64, 0:n], in1=sinq[:, o0:o0 + n], op=ALU.mult),
                                 reads=[psb[psi], b_tab, qrb], writes=[qrb])
                            S.op("dve", lambda e: e.tensor_tensor(out=qpe[:, 0:n], in0=qr[:, 0, 0:n], in1=qr[:, 1, 0:n], op=ALU.add),
                                 reads=[qrb], writes=[qpeb])

                            def bias_b(kb, m):
                                return biasmb[:, 0:1] if kb == 0 else 0.0

                            def part_b(m, kb):
                                return kb == 0 or (kb - 1) <= 4 * m + 3

                            def epib(qb, h=h, g=g):
                                k = ctr["ep"] % 2
                                acc = ps[PS_O[qb]]
                                S.op("dve", lambda e: e.reciprocal(out=est[:, k, 0:1], in_=acc[:, 128:129]),
                                     reads=[psb[PS_O[qb]]], writes=[estb[k]])

                                def mk(out_ap, out_buf):
                                    S.op("dve", lambda e: e.scalar_tensor_tensor(out=out_ap, in0=acc[:, 0:128], scalar=est[:, k, 0:1],
                                                                                 in1=gsil[:, qb, :], op0=ALU.mult, op1=ALU.mult),
                                         reads=[psb[PS_O[qb]], estb[k], gsb], writes=[out_buf])
                                finish_o(g, qb, H + h, mk)

                            attention_pass(
                                g,
                                [(lambda kb: kT[:, kb * 128:(kb + 1) * 128], lambda c0, c1: qT[:, c0:c1], []),
                                 (lambda kb: kpeT[0:64, kb * 128:(kb + 1) * 128], lambda c0, c1: qpe[0:64, c0:c1],
                                  [qpeb] + b_ckv)],
                                192.0 ** -0.5, bias_b, 4, part_b, epib)
                S.barrier()
        S.barrier()

        with contextlib.ExitStack() as p2:
            b_w2 = Buf("w2")
            wout = load_w(p2, "wout_sb", [128, 2 * H, D], wout_d[:, :, :], sl_w, b_w2)
            fnw = sb(p2, "fnw", [128, D], F32)
            sl_f = S.slot("fnw")
            S.dma("sp", sl_f, fnw[:, :], bc(fnw_d), writes=[b_w2])
            oin = [sb(p2, f"oin{i}", [128, 2 * H, 512], BF16) for i in range(2)]
            oinb = [Buf(f"oin{i}") for i in range(2)]
            sl_oi = [S.slot("oi0"), S.slot("oi1")]
            xr = [sb(p2, f"xr{i}", [128, D], F32) for i in range(2)]
            xrb = [Buf(f"xr{i}") for i in range(2)]
            sl_xr = [S.slot("xr0"), S.slot("xr1")]
            hh = [sb(p2, f"hh{i}", [128, D], F32) for i in range(2)]
            hhb = [Buf(f"hh{i}") for i in range(2)]
            yo = [sb(p2, f"yo{i}", [128, D], F32) for i in range(2)]
            yob = [Buf(f"yo{i}") for i in range(2)]
            sl_y = [S.slot("y0"), S.slot("y1")]
            j2 = sb(p2, "junk2", [128, D], BF16)
            j2b = Buf("junk2")
            st2 = sb(p2, "st2", [128, 2, 4], F32)
            st2b = [Buf("st2a"), Buf("st2b")]
            NCG = D // 512
            banks = PS_O + PS_S + PS_M
            bi_ = 0
            out_toks = []
            for g in range(NG):
                nbq = min(4, NOWN - 4 * g)
                gi = g % 2
                S.dma("sp", sl_oi[gi], oin[gi][:, :, 0:nbq * 128], oT_d[g, :, :, 0:nbq * 128], writes=[oinb[gi]])
                for blk in range(nbq):
                    ob = 4 * g + blk
                    k = ob % 2
                    S.dma("sp", sl_xr[k], xr[k][:, :], x_own[ob * 128:(ob + 1) * 128, :], writes=[xrb[k]])
                    for ng in range(NCG):
                        psi = banks[bi_ % 8]
                        bi_ += 1
                        for hd in range(2 * H):
                            mm(ps[psi][:, :], oin[gi][:, hd, blk * 128:(blk + 1) * 128], wout[:, hd, ng * 512:(ng + 1) * 512],
                               hd == 0, hd == 2 * H - 1, [oinb[gi], b_w2], [psb[psi]], signal=(hd == 2 * H - 1))
                        S.op("dve", lambda e: e.tensor_tensor(out=hh[k][:, ng * 512:(ng + 1) * 512], in0=ps[psi][:, :],
                                                              in1=xr[k][:, ng * 512:(ng + 1) * 512], op=ALU.add),
                             reads=[psb[psi], xrb[k]], writes=[hhb[k]])
                    sumsq(hh[k][:, :], j2[:, :], st2[:, k, 0:1], [hhb[k]], [j2b, st2b[k]])
                    rstd_from_ss(st2[:, k, 0:1], st2[:, k, 1:2], D, st2b[k])
                    S.op("dve", lambda e: e.scalar_tensor_tensor(out=yo[k][:, :], in0=hh[k][:, :], scalar=st2[:, k, 1:2],
                                                                 in1=fnw[:, :], op0=ALU.mult, op1=ALU.mult),
                         reads=[hhb[k], st2b[k], b_w2], writes=[yob[k]])
                    out_toks.append(S.dma("sp", sl_y[k], y_d[ob * 128:(ob + 1) * 128, :], yo[k][:, :], reads=[yob[k]]))
            for t in out_toks[-2:]:
                S._wait("sp", t)
            for t in out_toks[-2:]:
                S._wait("act", t)
    return nc


def _pc(a, C):
    n = a.shape[1]
    return np.ascontiguousarray(a.reshape(C, 128, n).transpose(1, 0, 2))


def make_in_maps(cfg, x, meta_tokens, attn_norm_w, w_in, diff_lambda, diff_subln_w,
                 mla_q_norm_w, w_uq, mla_kv_norm_w, w_ukv, w_out, final_norm_w):
    D, C, H, QL, KVL, S_ = cfg.D, cfg.C, cfg.H, cfg.QL, cfg.KVL, cfg.S
    NOWN, NKB, CQ, CKV = cfg.NOWN, cfg.NKB, cfg.CQ, cfg.CKV
    f32 = np.float32
    w_in = np.asarray(w_in, f32)[0]
    w_uq = np.asarray(w_uq, f32)[0]
    w_ukv = np.asarray(w_ukv, f32)[0]
    w_out = np.asarray(w_out, f32)[0]
    x = np.asarray(x, f32)
    AW = H * 128
    o_q, o_k, o_v, o_g = 0, AW, 2 * AW, 3 * AW
    o_cq = 4 * AW
    o_ckv = o_cq + QL
    o_kr = o_ckv + KVL
    o_bg = o_kr + 64
    wA = np.stack([_pc(np.concatenate([w_in[:, o_q + h * 128:o_q + (h + 1) * 128], w_in[:, o_k + h * 128:o_k + (h + 1) * 128],
                                       w_in[:, o_v + h * 128:o_v + (h + 1) * 128], w_in[:, o_g + h * 128:o_g + (h + 1) * 128]], axis=1), C)
                   for h in range(H)])
    wckvkr = _pc(w_in[:, o_ckv:o_ckv + KVL + 64], C)
    wcq = _pc(w_in[:, o_cq:o_cq + QL], C)
    wbg = np.stack([_pc(w_in[:, o_bg + h * 128:o_bg + (h + 1) * 128], C) for h in range(H)])
    perm = np.concatenate([np.arange(32, 64), np.arange(0, 32)])
    wuq = np.stack([_pc(np.concatenate([w_uq[:, h * 192:h * 192 + 128], w_uq[:, h * 192 + 128:h * 192 + 192],
                                        w_uq[:, h * 192 + 128:h * 192 + 192][:, perm]], axis=1), CQ) for h in range(H)])
    wukv = np.stack([_pc(w_ukv[:, h * 256:(h + 1) * 256], CKV) for h in range(H)])
    wout = _pc(w_out, 2 * H)
    meta_pad = np.zeros((128, D), f32)
    meta_pad[:16] = np.asarray(meta_tokens, f32)
    inv = (ROPE_THETA ** (-np.arange(32, dtype=np.float64) / 32.0))
    kpos = np.zeros((NKB, 128), np.float64)
    kpos[0] = np.arange(128)
    for kb in range(1, NKB):
        kpos[kb] = 16 + (kb - 1) * 128 + np.arange(128)
    angk = kpos[:, :, None] * inv[None, None, :]
    cosk = np.ascontiguousarray(np.cos(angk).transpose(1, 0, 2)).astype(f32)
    sink = np.ascontiguousarray(np.sin(angk).transpose(1, 0, 2)).astype(f32)
    ident = np.eye(128, dtype=f32)
    slopes = np.asarray(cfg.slopes, np.float64)
    kl = np.arange(128, dtype=np.float64)
    biasmb = np.where(kl < 16, 0.0, NEG).astype(f32).reshape(128, 1)
    maps = []
    for core in range(cfg.ncores):
        b, r = core // cfg.RB, core % cfg.RB
        own_blocks = [4 * m + r for m in range(NOWN)]
        x_own = np.concatenate([x[b, i * 128:(i + 1) * 128] for i in own_blocks], axis=0)
        qpos = np.concatenate([16 + i * 128 + np.arange(128) for i in own_blocks]).astype(np.float64)
        angq = qpos[None, :] * inv[:, None]
        cosq = np.concatenate([np.cos(angq), np.cos(angq)], axis=0).astype(f32)
        sinq = np.concatenate([-np.sin(angq), np.sin(angq)], axis=0).astype(f32)
        NBX = 4 * NOWN + 4
        biasx = np.zeros((128, H, NBX), f32)
        biasm = np.zeros((128, H, NOWN), f32)
        for h in range(H):
            for ddx in range(NBX):
                dd = ddx - 3
                biasx[:, h, ddx] = -slopes[h] * (128.0 * (dd + r) + 127.0 - kl)
            for m in range(NOWN):
                v = -slopes[h] * (16.0 + 128.0 * (4 * m + r) + 127.0 - kl)
                biasm[:, h, m] = np.where(kl < 16, v, NEG)
        masks = np.zeros((128, 4, 128), f32)
        for jj in range(4):
            if jj == r:
                masks[:, jj, :] = np.where(kl[:, None] <= kl[None, :], 0.0, NEG)
            elif jj > r:
                masks[:, jj, :] = NEG
        maps.append({
            "x_all": np.ascontiguousarray(x[b]), "x_own": np.ascontiguousarray(x_own), "meta_pad": meta_pad,
            "wA": wA, "wckvkr": wckvkr, "wcq": wcq, "wbg": wbg, "wuq": wuq, "wukv": wukv, "wout": wout,
            "cosk": cosk, "sink": sink, "cosq": cosq, "sinq": sinq,
            "biasx": biasx, "biasm": biasm, "biasmb": biasmb, "masks": masks, "ident": ident,
            "attn_norm_w": np.asarray(attn_norm_w, f32).reshape(1, D),
            "q_norm_w": np.asarray(mla_q_norm_w, f32).reshape(1, QL),
            "kv_norm_w": np.asarray(mla_kv_norm_w, f32).reshape(1, KVL),
            "final_norm_w": np.asarray(final_norm_w, f32).reshape(1, D),
            "subln_w": np.asarray(diff_subln_w, f32).reshape(1, 128),
            "diff_lambda": np.asarray(diff_lambda, f32).reshape(1, 256),
        })
    return maps


def gather_out(cfg, results):
    out = np.zeros((cfg.B, cfg.S, cfg.D), np.float32)
    for core in range(cfg.ncores):
        b, r = core // cfg.RB, core % cfg.RB
        y = np.asarray(results[core]["y_own"], np.float32)
        for m in range(cfg.NOWN):
            i = 4 * m + r
            out[b, i * 128:(i + 1) * 128] = y[m * 128:(m + 1) * 128]
    return out


def kernel(x, meta_tokens, attn_norm_w, w_in, diff_lambda, diff_subln_w,
           mla_q_norm_w, w_uq, mla_kv_norm_w, w_ukv, w_out, final_norm_w):
    cfg = Cfg()
    nc = build_nc(cfg)
    in_maps = make_in_maps(cfg, x, meta_tokens, attn_norm_w, w_in, diff_lambda, diff_subln_w,
                           mla_q_norm_w, w_uq, mla_kv_norm_w, w_ukv, w_out, final_norm_w)
    res = run_bass_kernel_spmd(nc, in_maps, core_ids=list(range(cfg.ncores)))
    return gather_out(cfg, res.results)
```
